# Optimizing a Trainium2 kernel written in Bass

```python
import jax, jax.numpy as jnp
from jax import lax
import numpy as np

D_MODEL = 1024
BATCH = 2
SEQ = 16384
DEPTH = 1
DEC_BATCH = 8
DEC_SEQ = 64
PAST_LEN = 1024

CHUNK = 64
D_MIX = D_MODEL
GLA_WIDTH = D_MIX // 2
GLA_HEADS = 4
GLA_DV = GLA_WIDTH // GLA_HEADS
GLA_DK = GLA_DV // 2
GLA_KEY = GLA_HEADS * GLA_DK
GATE_RANK = 16
GATE_NORMALIZER = 16.0
LOG_ALPHA_MIN = -4.0
GLA_BLOCK = 16
POOL_WIDTH = D_MIX - GLA_WIDTH
POOL_WINDOWS = (2, 4, 8, 16)
POOL_GROUPS = len(POOL_WINDOWS)
POOL_GC = POOL_WIDTH // POOL_GROUPS
POOL_PAST = max(POOL_WINDOWS) - 1
D_FF = ((8 * D_MODEL // 3 + 127) // 128) * 128
EPS = 1e-6
SPLIT_SIZES = (GLA_KEY, GLA_KEY, GLA_WIDTH, GLA_WIDTH, GATE_RANK, POOL_WIDTH)
D_IN = sum(SPLIT_SIZES)
SPLIT_POINTS = tuple(int(s) for s in np.cumsum(SPLIT_SIZES)[:-1])

kernel_name = "hybrid_gla_pool_macaron_stream_step"


def rmsnorm(x, g):
    xf = x.astype(jnp.float32)
    y = xf * lax.rsqrt(jnp.mean(xf * xf, axis=-1, keepdims=True) + EPS)
    return (y * g.astype(jnp.float32)).astype(x.dtype)


def swiglu(x, w1, w3, w2):
    return (jax.nn.silu(x @ w1) * (x @ w3)) @ w2


def gla_recurrence(q, k, v, log_a, s0):
    B, T = q.shape[0], q.shape[1]
    pad = (-T) % GLA_BLOCK
    if pad:
        padw = ((0, 0), (0, pad), (0, 0), (0, 0))
        q, k, v, log_a = [jnp.pad(a, padw) for a in (q, k, v, log_a)]
    nb = (T + pad) // GLA_BLOCK

    def blocks(a):
        return a.reshape(B, nb, GLA_BLOCK, GLA_HEADS, a.shape[-1]).transpose(1, 0, 3, 2, 4)

    qb, kb, vb, ab = blocks(q), blocks(k), blocks(v), blocks(log_a)
    causal = jnp.tril(jnp.ones((GLA_BLOCK, GLA_BLOCK), dtype=bool))

    def step(s, blk):
        qc, kc, vc, ac = blk
        b = jnp.cumsum(ac, axis=2)
        b_last = b[:, :, -1:, :]
        q_t = qc * jnp.exp(b)
        k_t = kc * jnp.exp(-b)
        attn = jnp.where(causal, jnp.einsum('bhik,bhjk->bhij', q_t, k_t), 0.0)
        o = (jnp.einsum('bhik,bhkv->bhiv', q_t, s)
             + jnp.einsum('bhij,bhjv->bhiv', attn, vc))
        k_end = kc * jnp.exp(b_last - b)
        s = (s * jnp.exp(b_last[:, :, 0, :])[..., None]
             + jnp.einsum('bhjk,bhjv->bhkv', k_end, vc))
        return s, o

    s_final, ob = lax.scan(step, s0, (qb, kb, vb, ab))
    o = ob.transpose(1, 0, 3, 2, 4).reshape(B, nb * GLA_BLOCK, GLA_HEADS, GLA_DV)[:, :T]
    return o, s_final


def pool_mix(u, past, pos0):
    T = u.shape[1]
    ext = jnp.concatenate([past, u], axis=1)
    cs = jnp.cumsum(jnp.pad(ext, ((0, 0), (1, 0), (0, 0))), axis=1)
    cur = cs[:, POOL_PAST + 1:]
    pos = pos0 + jnp.arange(T)
    outs = []
    for gi, w in enumerate(POOL_WINDOWS):
        sl = slice(gi * POOL_GC, (gi + 1) * POOL_GC)
        wsum = cur[..., sl] - cs[:, POOL_PAST + 1 - w:POOL_PAST + 1 - w + T, sl]
        cnt = jnp.minimum(w, pos + 1).astype(jnp.float32)[None, :, None]
        outs.append(wsum / cnt - u[..., sl])
    return jnp.stack(outs, axis=2), ext[:, -POOL_PAST:]


def token_mix(h, s0, past, pos0, w_in, w_gate_up, b_gate, gla_norm, w_pool, pool_scale, w_out):
    B, T, _ = h.shape
    p = (h @ w_in).astype(jnp.float32)
    q, k, v, g, r, u = jnp.split(p, SPLIT_POINTS, axis=-1)
    log_a = jax.nn.log_sigmoid(r @ w_gate_up.astype(jnp.float32) + b_gate.astype(jnp.float32))
    log_a = jnp.maximum(log_a / GATE_NORMALIZER, LOG_ALPHA_MIN).reshape(B, T, GLA_HEADS, GLA_DK)
    q = q.reshape(B, T, GLA_HEADS, GLA_DK) * (GLA_DK ** -0.5)
    k = k.reshape(B, T, GLA_HEADS, GLA_DK)
    v = v.reshape(B, T, GLA_HEADS, GLA_DV)
    o, s_new = gla_recurrence(q, k, v, log_a, s0.astype(jnp.float32))
    o = o * lax.rsqrt(jnp.mean(o * o, axis=-1, keepdims=True) + EPS) * gla_norm.astype(jnp.float32)
    o = (o * jax.nn.silu(g.reshape(B, T, GLA_HEADS, GLA_DV))).reshape(B, T, GLA_WIDTH)
    z, past_new = pool_mix(u, past.astype(jnp.float32), pos0)
    z = jnp.einsum('btgc,gcd->btgd', z, w_pool.astype(jnp.float32)).reshape(B, T, POOL_WIDTH)
    z = z * pool_scale.astype(jnp.float32)
    mixed = jnp.concatenate([o, z], axis=-1).astype(h.dtype) @ w_out
    return mixed, s_new, past_new


def encoder(x, state_gla, cache_pool, pos0, w_in, w_gate_up, b_gate, gla_norm, w_pool, pool_scale,
            w_out, norm_ffn1, w1_ffn1, w3_ffn1, w2_ffn1, norm_mix, norm_ffn2, w1_ffn2, w3_ffn2,
            w2_ffn2, norm_final):
    states, caches = [], []
    for l in range(DEPTH):
        x = x + 0.5 * swiglu(rmsnorm(x, norm_ffn1[l]), w1_ffn1[l], w3_ffn1[l], w2_ffn1[l])
        m, s_new, c_new = token_mix(rmsnorm(x, norm_mix[l]), state_gla[l], cache_pool[l], pos0,
                                    w_in[l], w_gate_up[l], b_gate[l], gla_norm[l], w_pool[l],
                                    pool_scale[l], w_out[l])
        x = x + m
        x = x + 0.5 * swiglu(rmsnorm(x, norm_ffn2[l]), w1_ffn2[l], w3_ffn2[l], w2_ffn2[l])
        states.append(s_new.astype(x.dtype))
        caches.append(c_new.astype(x.dtype))
    return rmsnorm(x, norm_final), jnp.stack(states), jnp.stack(caches)


def setup_inputs(seed: int = 0) -> dict:
    key = jax.random.key(seed)
    ks = jax.random.split(key, 24)

    def n(k, shape, s):
        return jax.random.normal(k, shape, jnp.float32) * s

    L = DEPTH
    return {
        "x_prompt": n(ks[0], (BATCH, SEQ, D_MODEL), 1.0),
        "x_sample": n(ks[1], (DEC_BATCH, DEC_SEQ, D_MODEL), 1.0),
        "state_gla": n(ks[2], (L, DEC_BATCH, GLA_HEADS, GLA_DK, GLA_DV), 0.5),
        "cache_pool": n(ks[3], (L, DEC_BATCH, POOL_PAST, POOL_WIDTH), 1.0),
        "w_in": n(ks[4], (L, D_MODEL, D_IN), D_MODEL ** -0.5),
        "w_gate_up": n(ks[5], (L, GATE_RANK, GLA_KEY), GATE_RANK ** -0.5),
        "b_gate": n(ks[6], (L, GLA_KEY), 0.1),
        "gla_norm": 1.0 + n(ks[7], (L, GLA_DV), 0.02),
        "w_pool": n(ks[8], (L, POOL_GROUPS, POOL_GC, POOL_GC), POOL_GC ** -0.5),
        "pool_scale": 1.0 + n(ks[9], (L, POOL_WIDTH), 0.02),
        "w_out": n(ks[10], (L, D_MIX, D_MODEL), D_MIX ** -0.5),
        "norm_ffn1": 1.0 + n(ks[11], (L, D_MODEL), 0.02),
        "w1_ffn1": n(ks[12], (L, D_MODEL, D_FF), D_MODEL ** -0.5),
        "w3_ffn1": n(ks[13], (L, D_MODEL, D_FF), D_MODEL ** -0.5),
        "w2_ffn1": n(ks[14], (L, D_FF, D_MODEL), D_FF ** -0.5),
        "norm_mix": 1.0 + n(ks[15], (L, D_MODEL), 0.02),
        "norm_ffn2": 1.0 + n(ks[16], (L, D_MODEL), 0.02),
        "w1_ffn2": n(ks[17], (L, D_MODEL, D_FF), D_MODEL ** -0.5),
        "w3_ffn2": n(ks[18], (L, D_MODEL, D_FF), D_MODEL ** -0.5),
        "w2_ffn2": n(ks[19], (L, D_FF, D_MODEL), D_FF ** -0.5),
        "norm_final": 1.0 + n(ks[20], (D_MODEL,), 0.02),
    }


def reference(x_prompt, x_sample, state_gla, cache_pool, w_in, w_gate_up, b_gate, gla_norm, w_pool,
              pool_scale, w_out, norm_ffn1, w1_ffn1, w3_ffn1, w2_ffn1, norm_mix, norm_ffn2, w1_ffn2,
              w3_ffn2, w2_ffn2, norm_final):
    B = x_prompt.shape[0]
    s0_prompt = jnp.zeros((DEPTH, B, GLA_HEADS, GLA_DK, GLA_DV), jnp.float32)
    past_prompt = jnp.zeros((DEPTH, B, POOL_PAST, POOL_WIDTH), jnp.float32)
    y_prompt, state_gla_prompt, cache_pool_prompt = encoder(
        x_prompt, s0_prompt, past_prompt, 0, w_in, w_gate_up, b_gate, gla_norm, w_pool, pool_scale,
        w_out, norm_ffn1, w1_ffn1, w3_ffn1, w2_ffn1, norm_mix, norm_ffn2, w1_ffn2, w3_ffn2, w2_ffn2,
        norm_final)
    y_sample, state_gla_sample, cache_pool_sample = encoder(
        x_sample, state_gla, cache_pool, PAST_LEN, w_in, w_gate_up, b_gate, gla_norm, w_pool,
        pool_scale, w_out, norm_ffn1, w1_ffn1, w3_ffn1, w2_ffn1, norm_mix, norm_ffn2, w1_ffn2,
        w3_ffn2, w2_ffn2, norm_final)
    return (y_prompt, y_sample, state_gla_prompt, cache_pool_prompt, state_gla_sample, cache_pool_sample)
```

```python
import numpy as np
from contextlib import ExitStack
import concourse.bass as bass
import concourse.mybir as mybir
from concourse.bass_utils import run_bass_kernel_spmd

F32 = mybir.dt.float32
F32R = mybir.dt.float32r
AF = mybir.ActivationFunctionType
ALU = mybir.AluOpType

D = 1024
DFF = 2816
NFC = DFF // 128
DIN = 2064
NCORE = 8
W_AG = 578
NCST = 72
BG_CHUNK = (13, 2, 15, 2, 2, 0)
C_G1, C_GM, C_G2, C_GF, C_BG, C_GLAN, C_PS, C_EPS, C_ONE, C_HM, C_BM, C_SEL, C_SELP, C_INVW, C_ZERO = \
    0, 8, 16, 24, 32, 34, 35, 39, 40, 41, 43, 51, 59, 67, 71


def comp_offsets(T):
    tg = min(T, 128)
    TG = T // tg
    nb = T // 16
    offs = {}
    o = 0
    for name, w in [("x1", 8 * T), ("qm", 4 * T), ("kt", 2 * T), ("ke", 2 * T), ("vtm", TG * 512),
                    ("gate", 4 * T), ("u", 4 * T), ("d16", 2 * nb)]:
        offs[name] = (o, w)
        o += w
    return offs, o


class Buf:
    __slots__ = ("name", "w", "r")

    def __init__(self, name):
        self.name = name
        self.w = None
        self.r = []


class Prog:
    ENG = ("pe", "act", "dve", "pool", "sp")

    def __init__(self, nc, es, dry=False):
        self.nc, self.es, self.dry = nc, es, dry
        self.q = {e: [] for e in self.ENG}
        self.sems = {}
        self.cnt = {}
        self.known = {e: {} for e in self.ENG}
        self.defer = None

    def capture(self, fn):
        assert self.defer is None
        self.defer = []
        fn()
        ops, self.defer = self.defer, None

        def gen():
            for a in ops:
                self.op(*a)
                yield
        return gen()

    def sem(self, key):
        if key not in self.cnt:
            self.cnt[key] = 0
            if not self.dry:
                self.sems[key] = self.es.enter_context(self.nc.semaphore(key))
        return key

    def op(self, eng, fn, reads=(), writes=(), chan=None):
        if self.defer is not None:
            self.defer.append((eng, fn, list(reads), list(writes), chan))
            return None
        deps = []
        for b in reads:
            if b.w:
                deps.append(b.w)
        for b in writes:
            if b.w:
                deps.append(b.w)
            deps.extend(b.r)
        if chan is not None:
            key = self.sem("d_" + chan)
            inc = 16
            if self.cnt[key] > 0:
                deps.append((key, self.cnt[key]))
        else:
            key = self.sem("e_" + eng)
            inc = 1
        kn = self.known[eng]
        best = {}
        for k, v in deps:
            if eng == "pe" and k == "e_pe":
                continue
            if v > kn.get(k, 0) and v > best.get(k, 0):
                best[k] = v
        waits = []
        for k, v in best.items():
            kn[k] = v
            waits.append((k, v))
        self.cnt[key] += inc
        ev = (key, self.cnt[key])
        for b in reads:
            b.r.append(ev)
        for b in writes:
            b.w = ev
            b.r = []
        self.q[eng].append((waits, fn, key, inc))
        return ev

    def fence(self, src, dst):
        evs = []
        for b in src:
            if b.w:
                evs.append(b.w)
            evs.extend(b.r)
        for b in dst:
            b.r.extend(evs)

    def emit(self):
        nc = self.nc
        block = self.es.enter_context(nc.Block())
        finals = [(k, v) for k, v in self.cnt.items() if v > 0]

        def run(e, name):
            for waits, fn, key, inc in self.q[name]:
                for k, v in waits:
                    e.wait_ge(self.sems[k], v)
                inst = fn(e)
                inst.then_inc(self.sems[key], inc)
            if name == "sp":
                for k, v in finals:
                    e.wait_ge(self.sems[k], v)

        @block.sync
        def _(e):
            run(e, "sp")

        @block.tensor
        def _(e):
            run(e, "pe")

        @block.scalar
        def _(e):
            run(e, "act")

        @block.vector
        def _(e):
            run(e, "dve")

        @block.gpsimd
        def _(e):
            run(e, "pool")


class WStream:
    NS, NR, PF = 2, 6, 4

    def __init__(self, P, stage, ring, plan, resolve):
        self.P, self.stage, self.ring, self.resolve = P, stage, ring, resolve
        self.plan = plan
        self.rec = []
        self.issued = 0
        self.cons = 0
        self.sb = [Buf("wst%d" % i) for i in range(self.NS)]
        self.rb = [Buf("wrg%d" % i) for i in range(self.NR)]

    def _issue(self, i):
        ap, shape = self.resolve(self.plan[i])
        ss, rs = i % self.NS, i % self.NR
        rg = self.ring[:, rs, :]
        self.P.op("sp", lambda e, o=rg, i_=ap: e.dma_start(out=o, in_=i_.bitcast(F32R)), writes=[self.rb[rs]], chan="wr%d" % rs)

    def get(self, desc):
        i = self.cons
        self.cons += 1
        ap, shape = self.resolve(desc)
        if self.plan is None:
            self.rec.append(desc)
            rs = i % self.NR
        else:
            while self.issued < min(len(self.plan), i + 1 + self.PF):
                self._issue(self.issued)
                self.issued += 1
            rs = i % self.NR
        v = self.ring[:, rs, :]
        if len(shape) == 3:
            v = v.rearrange("p (a b) -> p a b", a=shape[1])
        return v, self.rb[rs]


def build(n_pre, tiles, dry_plan=None, dry=False):
    nc = bass.Bass("TRN2", target_bir_lowering=False)
    nc.dge_precook = False
    NROW = sum(t[1] for t in tiles)
    TP = 512

    def din(name, shape):
        return nc.dram_tensor(name, list(shape), F32, kind="ExternalInput").ap()

    def dout(name, shape):
        return nc.dram_tensor(name, list(shape), F32, kind="ExternalOutput").ap()

    cst_d = din("cst", [128, NCST])
    ident_d = din("ident", [128, 128])
    xp_d = din("xp", [128, max(n_pre, 1) * TP * 8])
    x_d = din("x", [128, NROW * 8])
    NBLK = dict(w1a=NFC, w3a=NFC, w2a=NFC, w1b=NFC, w3b=NFC, w2b=NFC, w_in=17, w_out=8)
    dram_w = {k: din(k, [n, 128, 1024]) for k, n in NBLK.items()}
    wgu_d = din("wgu", [128, 2, 128])
    smask_d = din("smask", [128, 512])
    wpool_d = din("wpool", [128, 4, 128])
    cmask_d = din("cmask", [128, 128])
    invcnt_d = din("invcnt", [128, 4, 16])
    s0s_d = din("s0s", [128, 512])
    halos_d = din("halos", [128, 4, 16])
    y_d = dout("y", [128, NROW * 8])
    sfin_d = dout("sfin", [2, 128, 512])
    tail_d = dout("tail", [2, 128, 4, 15])

    with ExitStack() as es:
        P = Prog(nc, es, dry=dry)

        def sb(name, shape, dt=F32):
            return es.enter_context(nc.sbuf_tensor("sb_" + name, list(shape), dt))

        PS = [es.enter_context(nc.psum_tensor("ps%d" % i, [128, 512], F32)) for i in range(8)]
        PB = [Buf("ps%d" % i) for i in range(8)]

        cst = sb("cst", [128, NCST]); cstB = Buf("cst")
        ident = sb("ident", [128, 128]); identB = Buf("ident")
        ident_r = sb("ident_r", [128, 128], F32R); identrB = Buf("ident_r")
        ones_r = sb("ones_r", [128, 128], F32R); onesB = Buf("ones_r")
        ones32 = sb("ones32", [128, 128]); ones32B = Buf("ones32")
        wring = sb("wring", [128, WStream.NR, 1024], F32R)

        WIN_COLS = [1536, 0, 128, 256, 384, 1024, 1152, 1280, 1408, 1552, 1680, 1808, 1936]

        def resolve(desc):
            name = desc[1]
            if desc[0] == "cols":
                i = desc[2] // 128 if name != "w_in" else WIN_COLS.index(desc[2])
                return dram_w[name][i], (128, 8, 128)
            if desc[0] == "rows":
                return dram_w[name][13 + desc[2]], (128, 2, 512)
            j, half = desc[2], desc[3]
            return dram_w[name][2 * j + half], (128, 2, 512)
        WS = WStream(P, None, wring, dry_plan, resolve)
        xT = sb("xT", [128, 8, 512]); xTB = [Buf("xT%d" % c) for c in range(8)]
        hT = sb("hT", [128, 8, 512], F32R); hTB = [Buf("hT%d" % c) for c in range(8)]
        mixed, mixB = hT, hTB
        gh = sb("gh", [128, 8, 512], F32R); ghB = [Buf("gh%d" % c) for c in range(8)]
        sgs = sb("sgs", [128, 2, 512]); sgB = [Buf("sg0"), Buf("sg1")]
        sq = sb("sq", [128, 2, 512], F32R); sqB = [Buf("sq0"), Buf("sq1")]
        zz, zzB = sq, sqB
        rstd = sb("rstd", [128, 512]); rstdB = Buf("rstd")
        lnv, lnvB = rstd, rstdB
        NTMP = 7
        tmp = sb("tmp", [128, NTMP, 512]); tB = [Buf("tmp%d" % i) for i in range(NTMP)]
        wgu_r = sb("wgu_r", [128, 2, 128], F32R); wgurB = Buf("wgu_r")
        smask = sb("smask", [128, 512]); smaskB = Buf("smask")
        rj = sb("rj", [128, 512], F32R); rjB = Buf("rj")
        qk_sb = sb("qk_sb", [128, 4, 512])
        q_sb = qk_sb[:, 0:2, :]; qsB = [Buf("qs0"), Buf("qs1")]
        k_sb = qk_sb[:, 2:4, :]; ksB = [Buf("ks0"), Buf("ks1")]
        o_sb = qk_sb; osB = [Buf("o%d" % h) for h in range(4)]
        kem2 = [sb("kem_a", [128, 2, 8, 128], F32R), sb("kem_b", [128, 2, 8, 128], F32R)]
        kem2B = [Buf("kem_a"), Buf("kem_b")]
        qg = sb("qg", [128, 8, 512])
        qm = qg[:, 0:4, :]; qmB = [Buf("qm%d" % h) for h in range(4)]
        xT2 = qg; xT2B = [Buf("xT2_%d" % c) for c in range(8)]
        kt = sb("kt", [128, 2, 512]); ktB = [Buf("kt0"), Buf("kt1")]
        ke = sb("ke", [128, 2, 512], F32R); keB = [Buf("ke0"), Buf("ke1")]
        vtm = sb("vtm", [128, 4, 512], F32R); vtmB = [Buf("vtm%d" % g) for g in range(4)]
        gate = qg[:, 4:8, :]; gateB = [Buf("gate%d" % c) for c in range(4)]
        uext = sb("uext", [128, 4, 528]); uextB = Buf("uext")
        d16 = sb("d16", [128, 2, 32]); d16B = Buf("d16")
        dT = sb("dT", [128, 2]); dTB = Buf("dT")
        NSL = 9
        Sr = sb("Sr", [128, 2, NSL, 256]); SrB = [[Buf("S%d_%d" % (pr, s_)) for s_ in range(NSL)] for pr in range(2)]
        attm2 = [sb("attm_a", [128, 4, 128], F32R), sb("attm_b", [128, 4, 128], F32R)]
        attm2B = [[Buf("attm_a%d" % h) for h in range(4)], [Buf("attm_b%d" % h) for h in range(4)]]
        kTt, kTtB = attm2[0], attm2B[0]
        Ea = sb("Ea", [128, 528]); EaB = Buf("Ea")
        Eb = sb("Eb", [128, 528]); EbB = Buf("Eb")
        zt = sb("zt", [128, 16]); ztB = Buf("zt")
        halo = sb("halo", [128, 4, 16]); haloB = Buf("halo")
        wpool_r = sb("wpool_r", [128, 4, 128], F32R); wpoolrB = Buf("wpool_r")
        cmask = sb("cmask", [128, 128]); cmaskB = Buf("cmask")
        invcnt = sb("invcnt", [128, 4, 16]); invcntB = Buf("invcnt")
        halos = sb("halos", [128, 4, 16]); halosB = Buf("halos")
        sgg, onn = sgs[:, 0, :], sgs[:, 1, :]
        sggB, onnB = sgB[0], sgB[1]

        def cc(col, n=1):
            return cst[:, col:col + n]

        P.op("sp", lambda e: e.dma_start(out=cst[:], in_=cst_d), writes=[cstB], chan="c0")
        P.op("sp", lambda e: e.dma_start(out=ident[:], in_=ident_d), writes=[identB], chan="c1")
        P.op("sp", lambda e: e.dma_start(out=tmp[:, 0, 0:256].rearrange("p (a b) -> p a b", a=2), in_=wgu_d), writes=[tB[0]], chan="c2")
        P.op("sp", lambda e: e.dma_start(out=smask[:], in_=smask_d), writes=[smaskB], chan="c3")
        P.op("sp", lambda e: e.dma_start(out=tmp[:, 1, 0:512].rearrange("p (a b) -> p a b", a=4), in_=wpool_d), writes=[tB[1]], chan="c4")
        P.op("sp", lambda e: e.dma_start(out=cmask[:], in_=cmask_d), writes=[cmaskB], chan="c5")
        P.op("sp", lambda e: e.dma_start(out=invcnt[:], in_=invcnt_d), writes=[invcntB], chan="c6")
        P.op("sp", lambda e: e.dma_start(out=halos[:], in_=halos_d), writes=[halosB], chan="c8")
        P.op("dve", lambda e: e.tensor_copy(ident_r[:], ident[:]), reads=[identB], writes=[identrB])
        P.op("dve", lambda e: e.memset(ones32[:], 1.0), writes=[ones32B])
        P.op("dve", lambda e: e.tensor_copy(ones_r[:], ones32[:]), reads=[ones32B], writes=[onesB])
        P.op("dve", lambda e: e.tensor_copy(wgu_r[:], tmp[:, 0, 0:256].rearrange("p (a b) -> p a b", a=2)), reads=[tB[0]], writes=[wgurB])
        P.op("dve", lambda e: e.tensor_copy(wpool_r[:], tmp[:, 1, 0:512].rearrange("p (a b) -> p a b", a=4)), reads=[tB[1]], writes=[wpoolrB])
        P.op("dve", lambda e: e.memset(halo[:], 0.0), writes=[haloB])
        for pr in range(2):
            P.op("dve", lambda e, pr=pr: e.memset(Sr[:, pr, 0, :], 0.0), writes=[SrB[pr][0]])
        slot = [0, 0]

        evac_rr = [0]

        def evac_copy(out, in_, reads, writes):
            evac_rr[0] ^= 1
            if evac_rr[0]:
                P.op("act", lambda e: e.activation(out=out, in_=in_, func=AF.Copy), reads=reads, writes=writes)
            else:
                P.op("dve", lambda e: e.tensor_copy(out, in_), reads=reads, writes=writes)

        def rmsnorm(T, gcol, out_fn, outB, xt=None, xtB=None, stats=True, apply=True):
            xt = xT if xt is None else xt
            xtB = xTB if xtB is None else xtB
            pb = 0
            if apply and not stats:
                for c in range(8):
                    P.op("dve", lambda e, c=c: e.scalar_tensor_tensor(out_fn(c), xt[:, c, 0:T], cc(gcol + c), rstd[:, 0:T], ALU.mult, ALU.mult),
                         reads=[xtB[c], rstdB, cstB], writes=[outB[c]])
                return
            for c in range(8):
                s = c % 2
                P.op("act", lambda e, s=s, c=c: e.activation(out=sq[:, s, 0:T], in_=xt[:, c, 0:T], func=AF.Square),
                     reads=[xtB[c]], writes=[sqB[s]])
                P.op("pe", lambda e, s=s, c=c: e.matmul(PS[pb][:, 0:T], ones_r[:], sq[:, s, 0:T], start=(c == 0), stop=(c == 7)),
                     reads=[onesB, sqB[s]], writes=[PB[pb]])
            P.op("act", lambda e: e.activation(out=lnv[:, 0:T], in_=PS[pb][:, 0:T], func=AF.Ln, scale=1.0 / D, bias=cc(C_EPS)),
                 reads=[PB[pb], cstB], writes=[lnvB])
            P.op("act", lambda e: e.activation(out=rstd[:, 0:T], in_=lnv[:, 0:T], func=AF.Exp, scale=-0.5),
                 reads=[lnvB], writes=[rstdB])
            if not apply:
                return
            for c in range(8):
                P.op("dve", lambda e, c=c: e.scalar_tensor_tensor(out_fn(c), xt[:, c, 0:T], cc(gcol + c), rstd[:, 0:T], ALU.mult, ALU.mult),
                     reads=[xtB[c], rstdB, cstB], writes=[outB[c]])

        def ffn(T, gcol, w1d, w3d, w2d, bg=None, xt=None, xtB=None, stats_done=False):
            xt = xT if xt is None else xt
            xtB = xTB if xtB is None else xtB
            def tick():
                pass

            def tick_part(point):
                if bg is None:
                    return
                for _ in range(BG_CHUNK[point]):
                    if next(bg, "end") == "end":
                        break
            rmsnorm(T, gcol, lambda c: hT[:, c, 0:T], hTB, xt, xtB, stats=not stats_done)
            first = True
            for (f_lo, f_n) in ((0, 8), (8, 8), (16, 6)):
                for fi in range(f_n):
                    fc = f_lo + fi
                    s = fc % 2
                    blk1, b1B = WS.get(("cols", w1d, fc * 128))
                    blk3, b3B = WS.get(("cols", w3d, fc * 128))
                    pa, pbk = 2 * s, 2 * s + 1

                    def mm(e, blk, pp):
                        r = None
                        for kc in range(8):
                            r = e.matmul(PS[pp][:, 0:T], blk[:, kc, :], hT[:, kc, 0:T], start=(kc == 0), stop=(kc == 7))
                        return r
                    if first:
                        for kc in range(8):
                            P.op("pe", lambda e, kc=kc, blk=blk1, pp=pa: e.matmul(PS[pp][:, 0:T], blk[:, kc, :], hT[:, kc, 0:T], start=(kc == 0), stop=(kc == 7)),
                                 reads=[b1B, hTB[kc]], writes=[PB[pa]])
                        first = False
                    else:
                        P.op("pe", lambda e, blk=blk1, pp=pa, mm=mm: mm(e, blk, pp), reads=[b1B] + hTB, writes=[PB[pa]])
                    P.op("pe", lambda e, blk=blk3, pp=pbk, mm=mm: mm(e, blk, pp), reads=[b3B] + hTB, writes=[PB[pbk]])
                    P.op("act", lambda e, s=s, pp=pa: e.activation(out=sgs[:, s, 0:T], in_=PS[pp][:, 0:T], func=AF.Silu),
                         reads=[PB[pa]], writes=[sgB[s]])
                    P.op("dve", lambda e, s=s, pp=pbk, fi=fi: e.tensor_tensor(gh[:, fi, 0:T], sgs[:, s, 0:T], PS[pp][:, 0:T], ALU.mult),
                         reads=[sgB[s], PB[pbk]], writes=[ghB[fi]])
                    tick()
                tick_part(2 * (f_lo // 8))
                for half in range(2):
                    ab = 4 if half == 0 else 0
                    for jb in range(f_n // 2):
                        blk, bB = WS.get(("w2", w2d, f_lo // 2 + jb, half))

                        def mm2(e, blk=blk, jb=jb, f_n=f_n, ab=ab):
                            r = None
                            for f2 in range(2):
                                fi = 2 * jb + f2
                                for o4 in range(4):
                                    r = e.matmul(PS[ab + o4][:, 0:T], blk[:, f2, o4 * 128:(o4 + 1) * 128], gh[:, fi, 0:T],
                                                 start=(fi == 0), stop=(fi == f_n - 1))
                            return r
                        P.op("pe", mm2, reads=[bB, ghB[2 * jb], ghB[2 * jb + 1]], writes=PB[ab:ab + 4])
                        tick()
                    for o4 in range(4):
                        oc = half * 4 + o4
                        P.op("dve", lambda e, o4=o4, oc=oc, ab=ab: e.scalar_tensor_tensor(xt[:, oc, 0:T], PS[ab + o4][:, 0:T], 0.5, xt[:, oc, 0:T], ALU.mult, ALU.add),
                             reads=[PB[ab + o4], xtB[oc]], writes=[xtB[oc]])
                tick_part(2 * (f_lo // 8) + 1)
            if bg is not None:
                for _ in bg:
                    pass

        def load_x(src_d, r0, T, xt=None, xtB=None):
            xt = xT if xt is None else xt
            xtB = xTB if xtB is None else xtB
            P.op("sp", lambda e: e.dma_start(out=xt[:, :, 0:T], in_=src_d[:, r0 * 8:(r0 + T) * 8].rearrange("p (c t) -> p c t", c=8)),
                 writes=xtB, chan="xin")

        prr = [0]

        def proj(T, c0, evac, split=False):
            blk, bB = WS.get(("cols", "w_in", c0))
            pp = prr[0] % 4
            prr[0] += 1
            if split:
                for kc in range(8):
                    P.op("pe", lambda e, kc=kc: e.matmul(PS[pp][:, 0:T], blk[:, kc, :], hT[:, kc, 0:T], start=(kc == 0), stop=(kc == 7)),
                         reads=[bB, hTB[kc]], writes=[PB[pp]])
                evac(pp)
                return

            def mm(e, blk=blk, pp=pp):
                r = None
                for kc in range(8):
                    r = e.matmul(PS[pp][:, 0:T], blk[:, kc, :], hT[:, kc, 0:T], start=(kc == 0), stop=(kc == 7))
                return r
            P.op("pe", mm, reads=[bB] + hTB, writes=[PB[pp]])
            evac(pp)

        def v_tokmajor(T):
            tg = min(T, 128)
            TG = T // tg
            for kp in range(4):
                blk, bB = WS.get(("rows", "w_in", kp, 512))

                def mmv(e, blk=blk, kp=kp):
                    r = None
                    for g in range(TG):
                        for kk in range(2):
                            kc = 2 * kp + kk
                            r = e.matmul(PS[4 + g][0:tg, :], hT[:, kc, g * tg:(g + 1) * tg], blk[:, kk, :], start=(kc == 0), stop=(kc == 7))
                    return r
                P.op("pe", mmv, reads=[bB, hTB[2 * kp], hTB[2 * kp + 1]], writes=PB[4:4 + TG])
            for g in range(TG):
                evac_copy(vtm[0:tg, g, :], PS[4 + g][0:tg, :], [PB[4 + g]], [vtmB[g]])

        def log_decay(T, pr, pl=None):
            pl = 6 + pr if pl is None else pl
            t = lambda i: tmp[:, i, 0:T]
            bg = cc(C_BG + pr)
            P.op("pe", lambda e: e.matmul(PS[pl][:, 0:T], wgu_r[:, pr, :], rj[:, 0:T], start=True, stop=True),
                 reads=[wgurB, rjB], writes=[PB[pl]])
            P.op("act", lambda e: e.activation(out=t(0), in_=PS[pl][:, 0:T], func=AF.Identity, bias=bg),
                 reads=[PB[pl], cstB], writes=[tB[0]])
            P.op("dve", lambda e: e.tensor_scalar(t(1), t(0), 0.0, None, ALU.min), reads=[tB[0]], writes=[tB[1]])
            P.op("dve", lambda e: e.scalar_tensor_tensor(t(2), t(1), 2.0, t(0), ALU.mult, ALU.subtract), reads=[tB[0], tB[1]], writes=[tB[2]])
            P.op("act", lambda e: e.activation(out=t(3), in_=t(2), func=AF.Exp), reads=[tB[2]], writes=[tB[3]])
            P.op("act", lambda e: e.activation(out=t(2), in_=t(3), func=AF.Ln, bias=cc(C_ONE), scale=1.0), reads=[tB[3], cstB], writes=[tB[2]])
            P.op("dve", lambda e: e.tensor_tensor(t(0), t(1), t(2), ALU.subtract), reads=[tB[1], tB[2]], writes=[tB[0]])
            P.op("dve", lambda e: e.tensor_scalar(t(1), t(0), 1.0 / 16.0, -4.0, ALU.mult, ALU.max), reads=[tB[0]], writes=[tB[1]])

        pending = [None]
        own_prefetched = [False]
        stats_ready = [False]

        def prefix_tile(pi):
            T = TP
            tg, TG = 128, 4
            last = (pi == n_pre - 1)
            xt, xtB = (xT, xTB) if pi % 2 == 0 else (xT2, xT2B)
            if pi == 0:
                load_x(xp_d, 0, T, xt, xtB)
            if pi + 1 < n_pre:
                nxt = (xT, xTB) if (pi + 1) % 2 == 0 else (xT2, xT2B)
                load_x(xp_d, (pi + 1) * TP, T, nxt[0], nxt[1])
            elif (pi + 1) % 2 == 0:
                load_x(x_d, tiles[0][0], tiles[0][1])
                own_prefetched[0] = True
            nxt_buf = None
            if pi + 1 < n_pre:
                nxt_buf = (xT, xTB) if (pi + 1) % 2 == 0 else (xT2, xT2B)
            elif own_prefetched[0]:
                nxt_buf = (xT, xTB)
            ffn(T, C_G1, "w1a", "w3a", "w2a", bg=pending[0], xt=xt, xtB=xtB, stats_done=stats_ready[0])
            pending[0] = None
            stats_ready[0] = False
            rmsnorm(T, C_GM, lambda c: hT[:, c, 0:T], hTB, xt, xtB)
            proj(T, 1536, lambda pp: evac_copy(rj[:, 0:T], PS[pp][:, 0:T], [PB[pp]], [rjB]), split=True)
            for pr in range(2):
                proj(T, 256 + pr * 128, lambda pp, pr=pr: evac_copy(k_sb[:, pr, 0:T], PS[pp][:, 0:T], [PB[pp]], [ksB[pr]]))
            if nxt_buf is not None:
                rmsnorm(TP if pi + 1 < n_pre else tiles[0][1], C_G1, None, None, nxt_buf[0], nxt_buf[1], stats=True, apply=False)
                stats_ready[0] = True
            if last:
                for c in range(4):
                    proj(T, 1552 + c * 128, lambda pp, c=c: evac_copy(uext[:, c, 16:16 + T], PS[pp][:, 0:T], [PB[pp]], [uextB]))
                P.op("dve", lambda e: e.tensor_copy(halo[:], uext[:, :, T:T + 16]), reads=[uextB, haloB], writes=[haloB])
            v_tokmajor(T)

            def tail():
                t = lambda i: tmp[:, i, 0:T]
                for pr in range(2):
                    log_decay(T, pr, pl=pr)
                    P.op("dve", lambda e: e.tensor_tensor_scan(t(0), cc(C_ONE).to_broadcast([128, T]), t(1), 0.0, ALU.mult, ALU.add),
                         reads=[cstB, tB[1]], writes=[tB[0]])
                    P.op("dve", lambda e: e.tensor_tensor(t(2), tmp[:, 0, T - 1:T].to_broadcast([128, T]), t(0), ALU.subtract),
                         reads=[tB[0]], writes=[tB[2]])
                    P.op("act", lambda e: e.activation(out=t(3), in_=t(2), func=AF.Exp), reads=[tB[2]], writes=[tB[3]])
                    P.op("act", lambda e, pr=pr: e.activation(out=dT[:, pr:pr + 1], in_=tmp[:, 0, T - 1:T], func=AF.Exp),
                         reads=[tB[0]], writes=[dTB])
                    P.op("dve", lambda e, pr=pr: e.tensor_tensor(ke[:, pr, 0:T], k_sb[:, pr, 0:T], t(3), ALU.mult),
                         reads=[ksB[pr], tB[3]], writes=[keB[pr]])

                    def trk(e, pr=pr):
                        r = None
                        for g in range(TG):
                            r = e.matmul(PS[2][0:tg, g * 128:(g + 1) * 128], ke[:, pr, g * tg:(g + 1) * tg], ident_r[:], start=True, stop=True)
                        return r
                    P.op("pe", trk, reads=[keB[pr], identrB], writes=[PB[2]])
                    evac_copy(kTt[0:tg, 0:TG, :], PS[2][0:tg, 0:TG * 128].rearrange("p (g c) -> p g c", g=TG), [PB[2]], kTtB)

                    def mmu(e, pr=pr):
                        r = None
                        for g in range(TG):
                            r = e.matmul(PS[3][:, 0:256], kTt[0:tg, g, :], vtm[0:tg, g, pr * 256:(pr + 1) * 256], start=(g == 0), stop=(g == TG - 1))
                        return r
                    P.op("pe", mmu, reads=kTtB + vtmB[0:TG], writes=[PB[3]])
                    P.op("dve", lambda e, pr=pr: e.scalar_tensor_tensor(Sr[:, pr, 0, :], Sr[:, pr, 0, :], dT[:, pr:pr + 1], PS[3][:, 0:256], ALU.mult, ALU.add),
                         reads=[SrB[pr][0], dTB, PB[3]], writes=[SrB[pr][0]])
            pending[0] = P.capture(tail)

        for pi_ in range(n_pre):
            prefix_tile(pi_)

        def set_state(src, srcB):
            for pr in range(2):
                slot[pr] = 0
                P.op("dve", lambda e, pr=pr: e.tensor_copy(Sr[:, pr, 0, :], src[:, pr * 256:(pr + 1) * 256]),
                     reads=[srcB], writes=[SrB[pr][0]])

        def own_tile(ti, r0, T, kind):
            tg = min(T, 128)
            TG = T // tg
            nb = T // 16
            nbg = tg // 16
            first_of_kind = (ti == 0) or (tiles[ti - 1][2] != kind)
            last_of_kind = (ti + 1 == len(tiles)) or (tiles[ti + 1][2] != kind)
            L = 16 + T
            if ti == 0:
                P.fence(xT2B, qmB + gateB)
            if not (ti == 0 and own_prefetched[0]):
                load_x(x_d, r0, T)
            ffn(T, C_G1, "w1a", "w3a", "w2a", bg=pending[0], stats_done=stats_ready[0])
            pending[0] = None
            stats_ready[0] = False
            rmsnorm(T, C_GM, lambda c: hT[:, c, 0:T], hTB)
            if kind == "s" and first_of_kind:
                P.op("sp", lambda e: e.dma_start(out=tmp[:, 6, :], in_=s0s_d), writes=[tB[6]], chan="c7")
                set_state(tmp[:, 6, :], tB[6])
                P.op("dve", lambda e: e.tensor_copy(halo[:], halos[:]), reads=[halosB, haloB], writes=[haloB])
            proj(T, 1536, lambda pp: evac_copy(rj[:, 0:T], PS[pp][:, 0:T], [PB[pp]], [rjB]), split=True)
            P.fence(osB, qsB + ksB)
            t = lambda i: tmp[:, i, 0:T]

            def prep(pr):
                log_decay(T, pr)
                P.op("dve", lambda e: e.tensor_tensor_scan(t(2), smask[:, 0:T], t(1), 0.0, ALU.mult, ALU.add),
                     reads=[smaskB, tB[1]], writes=[tB[2]])
                b3 = tmp[:, 2, 0:T].rearrange("p (n l) -> p n l", l=16)
                P.op("act", lambda e: e.activation(out=t(3), in_=t(2), func=AF.Exp), reads=[tB[2]], writes=[tB[3]])
                P.op("act", lambda e: e.activation(out=t(4), in_=t(2), func=AF.Exp, scale=-1.0), reads=[tB[2]], writes=[tB[4]])
                P.op("dve", lambda e: e.tensor_tensor(tmp[:, 5, 0:T].rearrange("p (n l) -> p n l", l=16),
                                                      b3[:, :, 15:16].to_broadcast([128, nb, 16]), b3, ALU.subtract),
                     reads=[tB[2]], writes=[tB[5]])
                P.op("act", lambda e: e.activation(out=t(6), in_=t(5), func=AF.Exp), reads=[tB[5]], writes=[tB[6]])
                P.op("act", lambda e: e.activation(out=d16[:, pr, 0:nb], in_=b3[:, :, 15], func=AF.Exp),
                     reads=[tB[2]], writes=[d16B])
                for hl in range(2):
                    h = 2 * pr + hl
                    P.op("dve", lambda e, h=h, hl=hl: e.scalar_tensor_tensor(qm[:, h, 0:T], q_sb[:, pr, 0:T], cc(C_HM + hl), t(3), ALU.mult, ALU.mult),
                         reads=[qsB[pr], tB[3], cstB], writes=[qmB[h]])
                P.op("dve", lambda e: e.tensor_tensor(kt[:, pr, 0:T], k_sb[:, pr, 0:T], t(4), ALU.mult),
                     reads=[ksB[pr], tB[4]], writes=[ktB[pr]])
                P.op("dve", lambda e: e.tensor_tensor(ke[:, pr, 0:T], k_sb[:, pr, 0:T], t(6), ALU.mult),
                     reads=[ksB[pr], tB[6]], writes=[keB[pr]])
            for pr in range(2):
                proj(T, pr * 128, lambda pp, pr=pr: evac_copy(q_sb[:, pr, 0:T], PS[pp][:, 0:T], [PB[pp]], [qsB[pr]]))
                proj(T, 256 + pr * 128, lambda pp, pr=pr: evac_copy(k_sb[:, pr, 0:T], PS[pp][:, 0:T], [PB[pp]], [ksB[pr]]))
                prep(pr)
            v_tokmajor(T)
            for c in range(4):
                proj(T, 1024 + c * 128, lambda pp, c=c: P.op("act", lambda e: e.activation(out=gate[:, c, 0:T], in_=PS[pp][:, 0:T], func=AF.Silu),
                                                             reads=[PB[pp]], writes=[gateB[c]]))
            for c in range(4):
                proj(T, 1552 + c * 128, lambda pp, c=c: evac_copy(uext[:, c, 16:L], PS[pp][:, 0:T], [PB[pp]], [uextB]))
            P.op("dve", lambda e: e.tensor_copy(uext[:, :, 0:16], halo[:]), reads=[haloB, uextB], writes=[uextB])
            P.fence(qsB + ksB, osB)

            UB = [PB[0], PB[1], PB[4], PB[5]]
            US = [PS[0][:, 0:256], PS[1][:, 0:256], PS[4][:, 0:256], PS[5][:, 0:256]]
            ucnt = [0]
            sl_of = {}

            def kem_attn(g):
                ts0 = g * tg
                kb = g % 2
                kem, kemB, attm, attmB = kem2[kb], kem2B[kb], attm2[kb], attm2B[kb]

                def trk(e):
                    r = None
                    for pr in range(2):
                        r = e.matmul(PS[2][0:tg, pr * 128:(pr + 1) * 128], ke[:, pr, ts0:ts0 + tg], ident_r[:], start=True, stop=True)
                    return r
                P.op("pe", trk, reads=keB + [identrB], writes=[PB[2]])
                for pr in range(2):
                    for n in range(nbg):
                        P.op("act", lambda e, pr=pr, n=n: e.activation(out=kem[0:tg, pr, n, :], in_=PS[2][0:tg, pr * 128:(pr + 1) * 128], func=AF.Copy,
                                                                       scale=cst[0:tg, C_BM + n:C_BM + n + 1]),
                             reads=[PB[2], cstB], writes=[kemB])
                for h in range(4):
                    pr = h // 2
                    P.op("pe", lambda e, h=h, pr=pr: e.matmul(PS[3][0:tg, h * 128:h * 128 + tg], kt[:, pr, ts0:ts0 + tg], qm[:, h, ts0:ts0 + tg], start=True, stop=True),
                         reads=[ktB[pr], qmB[h]], writes=[PB[3]])
                for h in range(4):
                    P.op("dve", lambda e, h=h: e.tensor_tensor(attm[0:tg, h, 0:tg], PS[3][0:tg, h * 128:h * 128 + tg], cmask[0:tg, 0:tg], ALU.mult),
                         reads=[PB[3], cmaskB], writes=[attmB[h]])

            def chain(g):
                kb = g % 2
                kem, kemB = kem2[kb], kem2B[kb]
                sl0 = list(slot)
                sl_of[g] = sl0
                for n in range(nbg):
                    blk = g * nbg + n
                    for pr in range(2):
                        ui = ucnt[0] % 4
                        ucnt[0] += 1
                        s_cur = (sl0[pr] + n) % NSL
                        s_nxt = (sl0[pr] + n + 1) % NSL
                        P.op("pe", lambda e, pr=pr, n=n, ui=ui: e.matmul(US[ui], kem[0:tg, pr, n, :], vtm[0:tg, g, pr * 256:(pr + 1) * 256], start=True, stop=True),
                             reads=[kemB, vtmB[g]], writes=[UB[ui]])
                        P.op("dve", lambda e, pr=pr, ui=ui, s_cur=s_cur, s_nxt=s_nxt, blk=blk: e.scalar_tensor_tensor(
                            Sr[:, pr, s_nxt, :], Sr[:, pr, s_cur, :], d16[:, pr, blk:blk + 1], US[ui], ALU.mult, ALU.add),
                            reads=[SrB[pr][s_cur], d16B, UB[ui]], writes=[SrB[pr][s_nxt]])
                for pr in range(2):
                    slot[pr] = (sl0[pr] + nbg) % NSL

            def outp(g):
                ts0 = g * tg
                kb = g % 2
                attm, attmB = attm2[kb], attm2B[kb]
                sl0 = sl_of[g]
                for h in range(4):
                    pr, hl = h // 2, h % 2
                    po = 6 + h % 2

                    def mmo(e, h=h, pr=pr, hl=hl, po=po):
                        r = e.matmul(PS[po][:, 0:tg], vtm[0:tg, g, h * 128:(h + 1) * 128], attm[0:tg, h, 0:tg], start=True, stop=False)
                        for n in range(nbg):
                            s_n = (sl0[pr] + n) % NSL
                            r = e.matmul(PS[po][:, n * 16:(n + 1) * 16], Sr[:, pr, s_n, hl * 128:(hl + 1) * 128],
                                         qm[:, h, ts0 + n * 16:ts0 + (n + 1) * 16], start=False, stop=(n == nbg - 1))
                        return r
                    P.op("pe", mmo, reads=[vtmB[g], attmB[h], qmB[h]] + [SrB[pr][(sl0[pr] + n) % NSL] for n in range(nbg)], writes=[PB[po]])
                    evac_copy(o_sb[:, h, ts0:ts0 + tg], PS[po][:, 0:tg], [PB[po]], [osB[h]])
            kem_attn(0)
            for g_ in range(TG):
                chain(g_)
                if g_ + 1 < TG:
                    kem_attn(g_ + 1)
                outp(g_)
            for h in range(4):
                s = h % 2
                P.op("act", lambda e, h=h, s=s: e.activation(out=sq[:, s, 0:T], in_=o_sb[:, h, 0:T], func=AF.Square), reads=[osB[h]], writes=[sqB[s]])
                P.op("pe", lambda e, s=s: e.matmul(PS[0][:, 0:T], ones_r[:], sq[:, s, 0:T], start=True, stop=True), reads=[onesB, sqB[s]], writes=[PB[0]])
                P.op("act", lambda e: e.activation(out=lnv[:, 0:T], in_=PS[0][:, 0:T], func=AF.Ln, scale=1.0 / 128, bias=cc(C_EPS)), reads=[PB[0], cstB], writes=[lnvB])
                P.op("act", lambda e: e.activation(out=rstd[:, 0:T], in_=lnv[:, 0:T], func=AF.Exp, scale=-0.5), reads=[lnvB], writes=[rstdB])
                P.op("dve", lambda e, h=h: e.scalar_tensor_tensor(onn[:, 0:T], o_sb[:, h, 0:T], cc(C_GLAN), rstd[:, 0:T], ALU.mult, ALU.mult),
                     reads=[osB[h], rstdB, cstB], writes=[onnB])
                P.op("dve", lambda e, h=h: e.tensor_tensor(mixed[:, h, 0:T], onn[:, 0:T], gate[:, h, 0:T], ALU.mult), reads=[onnB, gateB[h]], writes=[mixB[h]])
            for gi in range(4):
                src, srcBuf = uext[:, gi, :], uextB
                for lv in range(gi + 1):
                    sh = 1 << lv
                    lo = 2 * sh - 1
                    dst, dstBuf = (Ea, EaB) if lv % 2 == 0 else (Eb, EbB)
                    P.op("dve", lambda e, src=src, dst=dst, sh=sh, lo=lo: e.tensor_tensor(dst[:, lo:L], src[:, lo:L], src[:, lo - sh:L - sh], ALU.add),
                         reads=[srcBuf], writes=[dstBuf])
                    src, srcBuf = dst, dstBuf
                zs = gi % 2
                P.op("dve", lambda e, gi=gi, src=src, zs=zs: e.scalar_tensor_tensor(zz[:, zs, 0:T], src[:, 16:L], cc(C_INVW + gi), uext[:, gi, 16:L], ALU.mult, ALU.subtract),
                     reads=[srcBuf, uextB, cstB], writes=[zzB[zs]])
                if kind == "p" and first_of_kind:
                    P.op("dve", lambda e, gi=gi, src=src: e.tensor_tensor(zt[:, :], src[:, 16:32], invcnt[:, gi, :], ALU.mult), reads=[srcBuf, invcntB], writes=[ztB])
                    P.op("dve", lambda e, gi=gi, zs=zs: e.tensor_tensor(zz[:, zs, 0:16], zt[:, :], uext[:, gi, 16:32], ALU.subtract), reads=[ztB, uextB], writes=[zzB[zs]])
                P.op("pe", lambda e, gi=gi, zs=zs: e.matmul(PS[gi % 4][:, 0:T], wpool_r[:, gi, :], zz[:, zs, 0:T], start=True, stop=True), reads=[wpoolrB, zzB[zs]], writes=[PB[gi % 4]])
                P.op("act", lambda e, gi=gi: e.activation(out=mixed[:, 4 + gi, 0:T], in_=PS[gi % 4][:, 0:T], func=AF.Copy, scale=cc(C_PS + gi)),
                     reads=[PB[gi % 4], cstB], writes=[mixB[4 + gi]])
            P.op("dve", lambda e: e.tensor_copy(halo[:], uext[:, :, T:T + 16]), reads=[uextB, haloB], writes=[haloB])
            if last_of_kind:
                ko = 0 if kind == "p" else 1
                P.op("sp", lambda e: e.dma_start(out=tail_d[ko], in_=halo[:, :, 1:16]), reads=[haloB], chan="otl")
                for pr in range(2):
                    s_now = slot[pr]
                    P.op("sp", lambda e, pr=pr, s_now=s_now: e.dma_start(out=sfin_d[ko, :, pr * 256:(pr + 1) * 256], in_=Sr[:, pr, s_now, :]),
                         reads=[SrB[pr][s_now]], chan="osf%d" % pr)
            for oc in range(8):
                blk, bB = WS.get(("cols", "w_out", oc * 128))
                pp = 4 + oc % 4

                def mmw(e, blk=blk, pp=pp):
                    r = None
                    for c in range(8):
                        r = e.matmul(PS[pp][:, 0:T], blk[:, c, :], mixed[:, c, 0:T], start=(c == 0), stop=(c == 7))
                    return r
                P.op("pe", mmw, reads=[bB] + mixB, writes=[PB[pp]])
                P.op("dve", lambda e, oc=oc, pp=pp: e.tensor_tensor(xT[:, oc, 0:T], PS[pp][:, 0:T], xT[:, oc, 0:T], ALU.add), reads=[PB[pp], xTB[oc]], writes=[xTB[oc]])
            ffn(T, C_G2, "w1b", "w3b", "w2b")
            yB = [osB[0], osB[1], osB[2], osB[3], uextB, uextB, uextB, uextB]
            P.fence(qsB + ksB, osB)
            yc = lambda c: qk_sb[:, c, 0:T] if c < 4 else uext[:, c - 4, 0:T]
            rmsnorm(T, C_GF, yc, yB)
            yv = y_d[:, r0 * 8:(r0 + T) * 8].rearrange("p (c t) -> p c t", c=8)
            P.op("sp", lambda e: e.dma_start(out=yv[:, 0:4, :], in_=qk_sb[:, :, 0:T]), reads=osB, chan="oy0")
            P.op("sp", lambda e: e.dma_start(out=yv[:, 4:8, :], in_=uext[:, :, 0:T]), reads=[uextB], chan="oy1")

        for ti_, (r0_, T_, kind_) in enumerate(tiles):
            own_tile(ti_, r0_, T_, kind_)
        if not dry:
            P.emit()
    return nc, WS.rec


_CACHE = {}


def get_prog(n_pre, tiles):
    key = (n_pre, tuple(tiles))
    if key not in _CACHE:
        _, plan = build(n_pre, tiles, dry_plan=None, dry=True)
        nc, _ = build(n_pre, tiles, dry_plan=plan, dry=False)
        _CACHE[key] = nc
    return _CACHE[key]

def make_consts(core, n_chunks, inp):
    cst = np.zeros((128, NCST), np.float32)
    f = lambda v: np.asarray(v, np.float32)
    cst[:, C_G1:C_G1 + 8] = f(inp["norm_ffn1"])[0].reshape(8, 128).T
    cst[:, C_GM:C_GM + 8] = f(inp["norm_mix"])[0].reshape(8, 128).T
    cst[:, C_G2:C_G2 + 8] = f(inp["norm_ffn2"])[0].reshape(8, 128).T
    cst[:, C_GF:C_GF + 8] = f(inp["norm_final"]).reshape(8, 128).T
    cst[:, C_BG:C_BG + 2] = f(inp["b_gate"])[0].reshape(2, 128).T
    cst[:, C_GLAN] = f(inp["gla_norm"])[0]
    cst[:, C_PS:C_PS + 4] = f(inp["pool_scale"])[0].reshape(4, 128).T
    cst[:, C_EPS] = 1e-6
    cst[:, C_ONE] = 1.0
    cst[0:64, C_HM] = 0.125
    cst[64:128, C_HM + 1] = 0.125
    for n in range(8):
        cst[16 * n:16 * (n + 1), C_BM + n] = 1.0
    cst[:, C_SEL + core] = 1.0
    if core % n_chunks != 0:
        cst[:, C_SELP + core - 1] = 1.0
    for gi, w in enumerate((2, 4, 8, 16)):
        cst[:, C_INVW + gi] = 1.0 / w
    return cst


def kernel(x_prompt, x_sample, state_gla, cache_pool, w_in, w_gate_up, b_gate, gla_norm, w_pool,
           pool_scale, w_out, norm_ffn1, w1_ffn1, w3_ffn1, w2_ffn1, norm_mix, norm_ffn2, w1_ffn2,
           w3_ffn2, w2_ffn2, norm_final):
    inp = dict(norm_ffn1=norm_ffn1, norm_mix=norm_mix, norm_ffn2=norm_ffn2, norm_final=norm_final,
               b_gate=b_gate, gla_norm=gla_norm, pool_scale=pool_scale)
    f = lambda v: np.ascontiguousarray(np.asarray(v, np.float32))
    x_prompt, x_sample = f(x_prompt), f(x_sample)
    B, S, _ = x_prompt.shape
    NB_S, TS, _ = x_sample.shape
    n_chunks = NCORE // B
    CH = S // n_chunks
    TP = 512
    tiles = [(i * TP, TP, "p") for i in range(CH // TP)] + [(CH, TS, "s")]
    n_pre = (n_chunks - 1) * CH // TP
    ident = np.eye(128, dtype=np.float32)
    smask = np.ones((128, 512), np.float32)
    smask[:, ::16] = 0.0
    jj, tt = np.meshgrid(np.arange(128), np.arange(128), indexing="ij")
    cmask = ((jj // 16 == tt // 16) & (jj <= tt)).astype(np.float32)
    wgu = np.zeros((128, 2, 128), np.float32)
    wgu[0:16] = f(w_gate_up)[0].reshape(16, 2, 128)
    wpool = np.ascontiguousarray(f(w_pool)[0].transpose(1, 0, 2))
    sg = f(state_gla)[0]
    cp = f(cache_pool)[0]
    nc = get_prog(n_pre, tiles)
    maps = []
    def blk_cols(w, c0):
        return w[:, c0:c0 + 128].reshape(8, 128, 128).transpose(1, 0, 2).reshape(128, 1024)

    def blk_rows(w, kp, c0):
        return w[kp * 256:(kp + 1) * 256, c0:c0 + 512].reshape(2, 128, 512).transpose(1, 0, 2).reshape(128, 1024)

    def blk_w2(w, j, half):
        return w[j * 256:(j + 1) * 256, half * 512:(half + 1) * 512].reshape(2, 128, 512).transpose(1, 0, 2).reshape(128, 1024)
    WIN_COLS = [1536, 0, 128, 256, 384, 1024, 1152, 1280, 1408, 1552, 1680, 1808, 1936]
    win = f(w_in)[0]
    wts = {}
    for nm, w in (("w1a", w1_ffn1), ("w3a", w3_ffn1), ("w1b", w1_ffn2), ("w3b", w3_ffn2)):
        w = f(w)[0]
        wts[nm] = np.ascontiguousarray(np.stack([blk_cols(w, fc * 128) for fc in range(NFC)], 0))
    for nm, w in (("w2a", w2_ffn1), ("w2b", w2_ffn2)):
        w = f(w)[0]
        wts[nm] = np.ascontiguousarray(np.stack([blk_w2(w, i // 2, i % 2) for i in range(NFC)], 0))
    wts["w_in"] = np.ascontiguousarray(np.stack([blk_cols(win, c0) for c0 in WIN_COLS] + [blk_rows(win, kp, 512) for kp in range(4)], 0))
    wts["w_out"] = np.ascontiguousarray(np.stack([blk_cols(f(w_out)[0], oc * 128) for oc in range(8)], 0))
    def to_fm(rows, tl):
        return np.ascontiguousarray(np.concatenate(
            [rows[r0:r0 + T].reshape(T, 8, 128).transpose(2, 1, 0).reshape(128, 8 * T) for (r0, T, _) in tl], 1))

    def from_fm(a, tl):
        return np.concatenate([a[:, r0 * 8:(r0 + T) * 8].reshape(128, 8, T).transpose(2, 1, 0).reshape(T, 1024) for (r0, T, _) in tl], 0)
    pre_tiles = [(i * TP, TP, "p") for i in range(max(n_pre, 1))]
    for c in range(NCORE):
        b, j = c // n_chunks, c % n_chunks
        x = to_fm(np.concatenate([x_prompt[b, j * CH:(j + 1) * CH], x_sample[c]], 0), tiles)
        xp = np.zeros((max(n_pre, 1) * TP, D), np.float32)
        if j > 0:
            xp[(n_chunks - 1 - j) * CH:(n_chunks - 1) * CH] = x_prompt[b, 0:j * CH]
        xp = to_fm(xp, pre_tiles)
        invcnt = np.zeros((128, 4, 16), np.float32)
        for gi, w in enumerate((2, 4, 8, 16)):
            pos = j * CH + np.arange(16)
            invcnt[:, gi, :] = 1.0 / np.minimum(w, pos + 1)
        s0s = np.zeros((128, 2, 2, 128), np.float32)
        for h in range(4):
            pr, hl = h // 2, h % 2
            s0s[hl * 64:(hl + 1) * 64, pr, hl, :] = sg[c, h]
        halos = np.zeros((128, 4, 16), np.float32)
        halos[:, :, 1:16] = cp[c].reshape(15, 4, 128).transpose(2, 1, 0)
        maps.append(dict(cst=make_consts(c, n_chunks, inp), ident=ident, xp=xp, x=x, wgu=wgu, smask=smask, wpool=wpool,
                         cmask=cmask, invcnt=invcnt, s0s=s0s.reshape(128, 512), halos=halos, **wts))
    res = run_bass_kernel_spmd(nc, maps, core_ids=list(range(NCORE))).results
    y_prompt = np.zeros((B, S, D), np.float32)
    y_sample = np.zeros((NB_S, TS, D), np.float32)
    st_p = np.zeros((1, B, 4, 64, 128), np.float32)
    cp_p = np.zeros((1, B, 15, 512), np.float32)
    st_s = np.zeros((1, NB_S, 4, 64, 128), np.float32)
    cp_s = np.zeros((1, NB_S, 15, 512), np.float32)

    def unstate(a):
        a = a.reshape(128, 2, 2, 128)
        return np.stack([a[(h % 2) * 64:(h % 2 + 1) * 64, h // 2, h % 2, :] for h in range(4)], 0)

    def untail(a):
        return a.transpose(2, 1, 0).reshape(15, 512)
    for c in range(NCORE):
        b, j = c // n_chunks, c % n_chunks
        y = from_fm(res[c]["y"], tiles)
        y_prompt[b, j * CH:(j + 1) * CH] = y[0:CH]
        y_sample[c] = y[CH:CH + TS]
        st_s[0, c] = unstate(res[c]["sfin"][1])
        cp_s[0, c] = untail(res[c]["tail"][1])
        if j == n_chunks - 1:
            st_p[0, b] = unstate(res[c]["sfin"][0])
            cp_p[0, b] = untail(res[c]["tail"][0])
    return (y_prompt, y_sample, st_p, cp_p, st_s, cp_s)
```

```python
import numpy as np
from contextlib import ExitStack
import concourse.bass as bass
import concourse.mybir as mybir
from concourse.bass_utils import run_bass_kernel_spmd

F32 = mybir.dt.float32
F32R = mybir.dt.float32r
AF = mybir.ActivationFunctionType
ALU = mybir.AluOpType

D = 1024
DFF = 2816
NFC = DFF // 128
DIN = 2064
NCORE = 8
W_AG = 578
NCST = 72
BG_CHUNK = (13, 2, 15, 2, 2, 0)
C_G1, C_GM, C_G2, C_GF, C_BG, C_GLAN, C_PS, C_EPS, C_ONE, C_HM, C_BM, C_SEL, C_SELP, C_INVW, C_ZERO = \
    0, 8, 16, 24, 32, 34, 35, 39, 40, 41, 43, 51, 59, 67, 71


def comp_offsets(T):
    tg = min(T, 128)
    TG = T // tg
    nb = T // 16
    offs = {}
    o = 0
    for name, w in [("x1", 8 * T), ("qm", 4 * T), ("kt", 2 * T), ("ke", 2 * T), ("vtm", TG * 512),
                    ("gate", 4 * T), ("u", 4 * T), ("d16", 2 * nb)]:
        offs[name] = (o, w)
        o += w
    return offs, o


class Buf:
    __slots__ = ("name", "w", "r")

    def __init__(self, name):
        self.name = name
        self.w = None
        self.r = []


class Prog:
    ENG = ("pe", "act", "dve", "pool", "sp")

    def __init__(self, nc, es, dry=False):
        self.nc, self.es, self.dry = nc, es, dry
        self.q = {e: [] for e in self.ENG}
        self.sems = {}
        self.cnt = {}
        self.known = {e: {} for e in self.ENG}
        self.defer = None

    def capture(self, fn):
        assert self.defer is None
        self.defer = []
        fn()
        ops, self.defer = self.defer, None

        def gen():
            for a in ops:
                self.op(*a)
                yield
        return gen()

    def sem(self, key):
        if key not in self.cnt:
            self.cnt[key] = 0
            if not self.dry:
                self.sems[key] = self.es.enter_context(self.nc.semaphore(key))
        return key

    def op(self, eng, fn, reads=(), writes=(), chan=None):
        if self.defer is not None:
            self.defer.append((eng, fn, list(reads), list(writes), chan))
            return None
        deps = []
        for b in reads:
            if b.w:
                deps.append(b.w)
        for b in writes:
            if b.w:
                deps.append(b.w)
            deps.extend(b.r)
        if chan is not None:
            key = self.sem("d_" + chan)
            inc = 16
            if self.cnt[key] > 0:
                deps.append((key, self.cnt[key]))
        else:
            key = self.sem("e_" + eng)
            inc = 1
        kn = self.known[eng]
        best = {}
        for k, v in deps:
            if eng == "pe" and k == "e_pe":
                continue
            if v > kn.get(k, 0) and v > best.get(k, 0):
                best[k] = v
        waits = []
        for k, v in best.items():
            kn[k] = v
            waits.append((k, v))
        self.cnt[key] += inc
        ev = (key, self.cnt[key])
        for b in reads:
            b.r.append(ev)
        for b in writes:
            b.w = ev
            b.r = []
        self.q[eng].append((waits, fn, key, inc))
        return ev

    def fence(self, src, dst):
        evs = []
        for b in src:
            if b.w:
                evs.append(b.w)
            evs.extend(b.r)
        for b in dst:
            b.r.extend(evs)

    def emit(self):
        nc = self.nc
        block = self.es.enter_context(nc.Block())
        finals = [(k, v) for k, v in self.cnt.items() if v > 0]

        def run(e, name):
            for waits, fn, key, inc in self.q[name]:
                for k, v in waits:
                    e.wait_ge(self.sems[k], v)
                inst = fn(e)
                inst.then_inc(self.sems[key], inc)
            if name == "sp":
                for k, v in finals:
                    e.wait_ge(self.sems[k], v)

        @block.sync
        def _(e):
            run(e, "sp")

        @block.tensor
        def _(e):
            run(e, "pe")

        @block.scalar
        def _(e):
            run(e, "act")

        @block.vector
        def _(e):
            run(e, "dve")

        @block.gpsimd
        def _(e):
            run(e, "pool")


class WStream:
    NS, NR, PF = 2, 6, 4

    def __init__(self, P, stage, ring, plan, resolve):
        self.P, self.stage, self.ring, self.resolve = P, stage, ring, resolve
        self.plan = plan
        self.rec = []
        self.issued = 0
        self.cons = 0
        self.sb = [Buf("wst%d" % i) for i in range(self.NS)]
        self.rb = [Buf("wrg%d" % i) for i in range(self.NR)]

    def _issue(self, i):
        ap, shape = self.resolve(self.plan[i])
        ss, rs = i % self.NS, i % self.NR
        rg = self.ring[:, rs, :]
        self.P.op("sp", lambda e, o=rg, i_=ap: e.dma_start(out=o, in_=i_.bitcast(F32R)), writes=[self.rb[rs]], chan="wr%d" % rs)

    def get(self, desc):
        i = self.cons
        self.cons += 1
        ap, shape = self.resolve(desc)
        if self.plan is None:
            self.rec.append(desc)
            rs = i % self.NR
        else:
            while self.issued < min(len(self.plan), i + 1 + self.PF):
                self._issue(self.issued)
                self.issued += 1
            rs = i % self.NR
        v = self.ring[:, rs, :]
        if len(shape) == 3:
            v = v.rearrange("p (a b) -> p a b", a=shape[1])
        return v, self.rb[rs]


def build(n_pre, tiles, dry_plan=None, dry=False):
    nc = bass.Bass("TRN2", target_bir_lowering=False)
    nc.dge_precook = False
    NROW = sum(t[1] for t in tiles)
    TP = 512

    def din(name, shape):
        return nc.dram_tensor(name, list(shape), F32, kind="ExternalInput").ap()

    def dout(name, shape):
        return nc.dram_tensor(name, list(shape), F32, kind="ExternalOutput").ap()

    cst_d = din("cst", [128, NCST])
    ident_d = din("ident", [128, 128])
    xp_d = din("xp", [128, max(n_pre, 1) * TP * 8])
    x_d = din("x", [128, NROW * 8])
    NBLK = dict(w1a=NFC, w3a=NFC, w2a=NFC, w1b=NFC, w3b=NFC, w2b=NFC, w_in=17, w_out=8)
    dram_w = {k: din(k, [n, 128, 1024]) for k, n in NBLK.items()}
    wgu_d = din("wgu", [128, 2, 128])
    smask_d = din("smask", [128, 512])
    wpool_d = din("wpool", [128, 4, 128])
    cmask_d = din("cmask", [128, 128])
    invcnt_d = din("invcnt", [128, 4, 16])
    s0s_d = din("s0s", [128, 512])
    halos_d = din("halos", [128, 4, 16])
    y_d = dout("y", [128, NROW * 8])
    sfin_d = dout("sfin", [2, 128, 512])
    tail_d = dout("tail", [2, 128, 4, 15])

    with ExitStack() as es:
        P = Prog(nc, es, dry=dry)

        def sb(name, shape, dt=F32):
            return es.enter_context(nc.sbuf_tensor("sb_" + name, list(shape), dt))

        PS = [es.enter_context(nc.psum_tensor("ps%d" % i, [128, 512], F32)) for i in range(8)]
        PB = [Buf("ps%d" % i) for i in range(8)]

        cst = sb("cst", [128, NCST]); cstB = Buf("cst")
        ident = sb("ident", [128, 128]); identB = Buf("ident")
        ident_r = sb("ident_r", [128, 128], F32R); identrB = Buf("ident_r")
        ones_r = sb("ones_r", [128, 128], F32R); onesB = Buf("ones_r")
        ones32 = sb("ones32", [128, 128]); ones32B = Buf("ones32")
        wring = sb("wring", [128, WStream.NR, 1024], F32R)

        WIN_COLS = [1536, 0, 128, 256, 384, 1024, 1152, 1280, 1408, 1552, 1680, 1808, 1936]

        def resolve(desc):
            name = desc[1]
            if desc[0] == "cols":
                i = desc[2] // 128 if name != "w_in" else WIN_COLS.index(desc[2])
                return dram_w[name][i], (128, 8, 128)
            if desc[0] == "rows":
                return dram_w[name][13 + desc[2]], (128, 2, 512)
            j, half = desc[2], desc[3]
            return dram_w[name][2 * j + half], (128, 2, 512)
        WS = WStream(P, None, wring, dry_plan, resolve)
        xT = sb("xT", [128, 8, 512]); xTB = [Buf("xT%d" % c) for c in range(8)]
        hT = sb("hT", [128, 8, 512], F32R); hTB = [Buf("hT%d" % c) for c in range(8)]
        mixed, mixB = hT, hTB
        gh = sb("gh", [128, 8, 512], F32R); ghB = [Buf("gh%d" % c) for c in range(8)]
        sgs = sb("sgs", [128, 2, 512]); sgB = [Buf("sg0"), Buf("sg1")]
        sq = sb("sq", [128, 2, 512], F32R); sqB = [Buf("sq0"), Buf("sq1")]
        zz, zzB = sq, sqB
        rstd = sb("rstd", [128, 512]); rstdB = Buf("rstd")
        lnv, lnvB = rstd, rstdB
        NTMP = 7
        tmp = sb("tmp", [128, NTMP, 512]); tB = [Buf("tmp%d" % i) for i in range(NTMP)]
        wgu_r = sb("wgu_r", [128, 2, 128], F32R); wgurB = Buf("wgu_r")
        smask = sb("smask", [128, 512]); smaskB = Buf("smask")
        rj = sb("rj", [128, 512], F32R); rjB = Buf("rj")
        qk_sb = sb("qk_sb", [128, 4, 512])
        q_sb = qk_sb[:, 0:2, :]; qsB = [Buf("qs0"), Buf("qs1")]
        k_sb = qk_sb[:, 2:4, :]; ksB = [Buf("ks0"), Buf("ks1")]
        o_sb = qk_sb; osB = [Buf("o%d" % h) for h in range(4)]
        kem2 = [sb("kem_a", [128, 2, 8, 128], F32R), sb("kem_b", [128, 2, 8, 128], F32R)]
        kem2B = [Buf("kem_a"), Buf("kem_b")]
        qg = sb("qg", [128, 8, 512])
        qm = qg[:, 0:4, :]; qmB = [Buf("qm%d" % h) for h in range(4)]
        xT2 = qg; xT2B = [Buf("xT2_%d" % c) for c in range(8)]
        kt = sb("kt", [128, 2, 512]); ktB = [Buf("kt0"), Buf("kt1")]
        ke = sb("ke", [128, 2, 512], F32R); keB = [Buf("ke0"), Buf("ke1")]
        vtm = sb("vtm", [128, 4, 512], F32R); vtmB = [Buf("vtm%d" % g) for g in range(4)]
        gate = qg[:, 4:8, :]; gateB = [Buf("gate%d" % c) for c in range(4)]
        uext = sb("uext", [128, 4, 528]); uextB = Buf("uext")
        d16 = sb("d16", [128, 2, 32]); d16B = Buf("d16")
        dT = sb("dT", [128, 2]); dTB = Buf("dT")
        NSL = 9
        Sr = sb("Sr", [128, 2, NSL, 256]); SrB = [[Buf("S%d_%d" % (pr, s_)) for s_ in range(NSL)] for pr in range(2)]
        attm2 = [sb("attm_a", [128, 4, 128], F32R), sb("attm_b", [128, 4, 128], F32R)]
        attm2B = [[Buf("attm_a%d" % h) for h in range(4)], [Buf("attm_b%d" % h) for h in range(4)]]
        kTt, kTtB = attm2[0], attm2B[0]
        Ea = sb("Ea", [128, 528]); EaB = Buf("Ea")
        Eb = sb("Eb", [128, 528]); EbB = Buf("Eb")
        zt = sb("zt", [128, 16]); ztB = Buf("zt")
        halo = sb("halo", [128, 4, 16]); haloB = Buf("halo")
        wpool_r = sb("wpool_r", [128, 4, 128], F32R); wpoolrB = Buf("wpool_r")
        cmask = sb("cmask", [128, 128]); cmaskB = Buf("cmask")
        invcnt = sb("invcnt", [128, 4, 16]); invcntB = Buf("invcnt")
        halos = sb("halos", [128, 4, 16]); halosB = Buf("halos")
        sgg, onn = sgs[:, 0, :], sgs[:, 1, :]
        sggB, onnB = sgB[0], sgB[1]

        def cc(col, n=1):
            return cst[:, col:col + n]

        P.op("sp", lambda e: e.dma_start(out=cst[:], in_=cst_d), writes=[cstB], chan="c0")
        P.op("sp", lambda e: e.dma_start(out=ident[:], in_=ident_d), writes=[identB], chan="c1")
        P.op("sp", lambda e: e.dma_start(out=tmp[:, 0, 0:256].rearrange("p (a b) -> p a b", a=2), in_=wgu_d), writes=[tB[0]], chan="c2")
        P.op("sp", lambda e: e.dma_start(out=smask[:], in_=smask_d), writes=[smaskB], chan="c3")
        P.op("sp", lambda e: e.dma_start(out=tmp[:, 1, 0:512].rearrange("p (a b) -> p a b", a=4), in_=wpool_d), writes=[tB[1]], chan="c4")
        P.op("sp", lambda e: e.dma_start(out=cmask[:], in_=cmask_d), writes=[cmaskB], chan="c5")
        P.op("sp", lambda e: e.dma_start(out=invcnt[:], in_=invcnt_d), writes=[invcntB], chan="c6")
        P.op("sp", lambda e: e.dma_start(out=halos[:], in_=halos_d), writes=[halosB], chan="c8")
        P.op("dve", lambda e: e.tensor_copy(ident_r[:], ident[:]), reads=[identB], writes=[identrB])
        P.op("dve", lambda e: e.memset(ones32[:], 1.0), writes=[ones32B])
        P.op("dve", lambda e: e.tensor_copy(ones_r[:], ones32[:]), reads=[ones32B], writes=[onesB])
        P.op("dve", lambda e: e.tensor_copy(wgu_r[:], tmp[:, 0, 0:256].rearrange("p (a b) -> p a b", a=2)), reads=[tB[0]], writes=[wgurB])
        P.op("dve", lambda e: e.tensor_copy(wpool_r[:], tmp[:, 1, 0:512].rearrange("p (a b) -> p a b", a=4)), reads=[tB[1]], writes=[wpoolrB])
        P.op("dve", lambda e: e.memset(halo[:], 0.0), writes=[haloB])
        for pr in range(2):
            P.op("dve", lambda e, pr=pr: e.memset(Sr[:, pr, 0, :], 0.0), writes=[SrB[pr][0]])
        slot = [0, 0]

        evac_rr = [0]

        def evac_copy(out, in_, reads, writes):
            evac_rr[0] ^= 1
            if evac_rr[0]:
                P.op("act", lambda e: e.activation(out=out, in_=in_, func=AF.Copy), reads=reads, writes=writes)
            else:
                P.op("dve", lambda e: e.tensor_copy(out, in_), reads=reads, writes=writes)

        def rmsnorm(T, gcol, out_fn, outB, xt=None, xtB=None, stats=True, apply=True):
            xt = xT if xt is None else xt
            xtB = xTB if xtB is None else xtB
            pb = 0
            if apply and not stats:
                for c in range(8):
                    P.op("dve", lambda e, c=c: e.scalar_tensor_tensor(out_fn(c), xt[:, c, 0:T], cc(gcol + c), rstd[:, 0:T], ALU.mult, ALU.mult),
                         reads=[xtB[c], rstdB, cstB], writes=[outB[c]])
                return
            for c in range(8):
                s = c % 2
                P.op("act", lambda e, s=s, c=c: e.activation(out=sq[:, s, 0:T], in_=xt[:, c, 0:T], func=AF.Square),
                     reads=[xtB[c]], writes=[sqB[s]])
                P.op("pe", lambda e, s=s, c=c: e.matmul(PS[pb][:, 0:T], ones_r[:], sq[:, s, 0:T], start=(c == 0), stop=(c == 7)),
                     reads=[onesB, sqB[s]], writes=[PB[pb]])
            P.op("act", lambda e: e.activation(out=lnv[:, 0:T], in_=PS[pb][:, 0:T], func=AF.Ln, scale=1.0 / D, bias=cc(C_EPS)),
                 reads=[PB[pb], cstB], writes=[lnvB])
            P.op("act", lambda e: e.activation(out=rstd[:, 0:T], in_=lnv[:, 0:T], func=AF.Exp, scale=-0.5),
                 reads=[lnvB], writes=[rstdB])
            if not apply:
                return
            for c in range(8):
                P.op("dve", lambda e, c=c: e.scalar_tensor_tensor(out_fn(c), xt[:, c, 0:T], cc(gcol + c), rstd[:, 0:T], ALU.mult, ALU.mult),
                     reads=[xtB[c], rstdB, cstB], writes=[outB[c]])

        def ffn(T, gcol, w1d, w3d, w2d, bg=None, xt=None, xtB=None, stats_done=False):
            xt = xT if xt is None else xt
            xtB = xTB if xtB is None else xtB
            def tick():
                pass

            def tick_part(point):
                if bg is None:
                    return
                for _ in range(BG_CHUNK[point]):
                    if next(bg, "end") == "end":
                        break
            rmsnorm(T, gcol, lambda c: hT[:, c, 0:T], hTB, xt, xtB, stats=not stats_done)
            first = True
            for (f_lo, f_n) in ((0, 8), (8, 8), (16, 6)):
                for fi in range(f_n):
                    fc = f_lo + fi
                    s = fc % 2
                    blk1, b1B = WS.get(("cols", w1d, fc * 128))
                    blk3, b3B = WS.get(("cols", w3d, fc * 128))
                    pa, pbk = 2 * s, 2 * s + 1

                    def mm(e, blk, pp):
                        r = None
                        for kc in range(8):
                            r = e.matmul(PS[pp][:, 0:T], blk[:, kc, :], hT[:, kc, 0:T], start=(kc == 0), stop=(kc == 7))
                        return r
                    if first:
                        for kc in range(8):
                            P.op("pe", lambda e, kc=kc, blk=blk1, pp=pa: e.matmul(PS[pp][:, 0:T], blk[:, kc, :], hT[:, kc, 0:T], start=(kc == 0), stop=(kc == 7)),
                                 reads=[b1B, hTB[kc]], writes=[PB[pa]])
                        first = False
                    else:
                        P.op("pe", lambda e, blk=blk1, pp=pa, mm=mm: mm(e, blk, pp), reads=[b1B] + hTB, writes=[PB[pa]])
                    P.op("pe", lambda e, blk=blk3, pp=pbk, mm=mm: mm(e, blk, pp), reads=[b3B] + hTB, writes=[PB[pbk]])
                    P.op("act", lambda e, s=s, pp=pa: e.activation(out=sgs[:, s, 0:T], in_=PS[pp][:, 0:T], func=AF.Silu),
                         reads=[PB[pa]], writes=[sgB[s]])
                    P.op("dve", lambda e, s=s, pp=pbk, fi=fi: e.tensor_tensor(gh[:, fi, 0:T], sgs[:, s, 0:T], PS[pp][:, 0:T], ALU.mult),
                         reads=[sgB[s], PB[pbk]], writes=[ghB[fi]])
                    tick()
                tick_part(2 * (f_lo // 8))
                for half in range(2):
                    ab = 4 if half == 0 else 0
                    for jb in range(f_n // 2):
                        blk, bB = WS.get(("w2", w2d, f_lo // 2 + jb, half))

                        def mm2(e, blk=blk, jb=jb, f_n=f_n, ab=ab):
                            r = None
                            for f2 in range(2):
                                fi = 2 * jb + f2
                                for o4 in range(4):
                                    r = e.matmul(PS[ab + o4][:, 0:T], blk[:, f2, o4 * 128:(o4 + 1) * 128], gh[:, fi, 0:T],
                                                 start=(fi == 0), stop=(fi == f_n - 1))
                            return r
                        P.op("pe", mm2, reads=[bB, ghB[2 * jb], ghB[2 * jb + 1]], writes=PB[ab:ab + 4])
                        tick()
                    for o4 in range(4):
                        oc = half * 4 + o4
                        P.op("dve", lambda e, o4=o4, oc=oc, ab=ab: e.scalar_tensor_tensor(xt[:, oc, 0:T], PS[ab + o4][:, 0:T], 0.5, xt[:, oc, 0:T], ALU.mult, ALU.add),
                             reads=[PB[ab + o4], xtB[oc]], writes=[xtB[oc]])
                tick_part(2 * (f_lo // 8) + 1)
            if bg is not None:
                for _ in bg:
                    pass

        def load_x(src_d, r0, T, xt=None, xtB=None):
            xt = xT if xt is None else xt
            xtB = xTB if xtB is None else xtB
            P.op("sp", lambda e: e.dma_start(out=xt[:, :, 0:T], in_=src_d[:, r0 * 8:(r0 + T) * 8].rearrange("p (c t) -> p c t", c=8)),
                 writes=xtB, chan="xin")

        prr = [0]

        def proj(T, c0, evac, split=False):
            blk, bB = WS.get(("cols", "w_in", c0))
            pp = prr[0] % 4
            prr[0] += 1
            if split:
                for kc in range(8):
                    P.op("pe", lambda e, kc=kc: e.matmul(PS[pp][:, 0:T], blk[:, kc, :], hT[:, kc, 0:T], start=(kc == 0), stop=(kc == 7)),
                         reads=[bB, hTB[kc]], writes=[PB[pp]])
                evac(pp)
                return

            def mm(e, blk=blk, pp=pp):
                r = None
                for kc in range(8):
                    r = e.matmul(PS[pp][:, 0:T], blk[:, kc, :], hT[:, kc, 0:T], start=(kc == 0), stop=(kc == 7))
                return r
            P.op("pe", mm, reads=[bB] + hTB, writes=[PB[pp]])
            evac(pp)

        def v_tokmajor(T):
            tg = min(T, 128)
            TG = T // tg
            for kp in range(4):
                blk, bB = WS.get(("rows", "w_in", kp, 512))

                def mmv(e, blk=blk, kp=kp):
                    r = None
                    for g in range(TG):
                        for kk in range(2):
                            kc = 2 * kp + kk
                            r = e.matmul(PS[4 + g][0:tg, :], hT[:, kc, g * tg:(g + 1) * tg], blk[:, kk, :], start=(kc == 0), stop=(kc == 7))
                    return r
                P.op("pe", mmv, reads=[bB, hTB[2 * kp], hTB[2 * kp + 1]], writes=PB[4:4 + TG])
            for g in range(TG):
                evac_copy(vtm[0:tg, g, :], PS[4 + g][0:tg, :], [PB[4 + g]], [vtmB[g]])

        def log_decay(T, pr, pl=None):
            pl = 6 + pr if pl is None else pl
            t = lambda i: tmp[:, i, 0:T]
            bg = cc(C_BG + pr)
            P.op("pe", lambda e: e.matmul(PS[pl][:, 0:T], wgu_r[:, pr, :], rj[:, 0:T], start=True, stop=True),
                 reads=[wgurB, rjB], writes=[PB[pl]])
            P.op("act", lambda e: e.activation(out=t(0), in_=PS[pl][:, 0:T], func=AF.Identity, bias=bg),
                 reads=[PB[pl], cstB], writes=[tB[0]])
            P.op("dve", lambda e: e.tensor_scalar(t(1), t(0), 0.0, None, ALU.min), reads=[tB[0]], writes=[tB[1]])
            P.op("dve", lambda e: e.scalar_tensor_tensor(t(2), t(1), 2.0, t(0), ALU.mult, ALU.subtract), reads=[tB[0], tB[1]], writes=[tB[2]])
            P.op("act", lambda e: e.activation(out=t(3), in_=t(2), func=AF.Exp), reads=[tB[2]], writes=[tB[3]])
            P.op("act", lambda e: e.activation(out=t(2), in_=t(3), func=AF.Ln, bias=cc(C_ONE), scale=1.0), reads=[tB[3], cstB], writes=[tB[2]])
            P.op("dve", lambda e: e.tensor_tensor(t(0), t(1), t(2), ALU.subtract), reads=[tB[1], tB[2]], writes=[tB[0]])
            P.op("dve", lambda e: e.tensor_scalar(t(1), t(0), 1.0 / 16.0, -4.0, ALU.mult, ALU.max), reads=[tB[0]], writes=[tB[1]])

        pending = [None]
        own_prefetched = [False]
        stats_ready = [False]

        def prefix_tile(pi):
            T = TP
            tg, TG = 128, 4
            last = (pi == n_pre - 1)
            xt, xtB = (xT, xTB) if pi % 2 == 0 else (xT2, xT2B)
            if pi == 0:
                load_x(xp_d, 0, T, xt, xtB)
            if pi + 1 < n_pre:
                nxt = (xT, xTB) if (pi + 1) % 2 == 0 else (xT2, xT2B)
                load_x(xp_d, (pi + 1) * TP, T, nxt[0], nxt[1])
            elif (pi + 1) % 2 == 0:
                load_x(x_d, tiles[0][0], tiles[0][1])
                own_prefetched[0] = True
            nxt_buf = None
            if pi + 1 < n_pre:
                nxt_buf = (xT, xTB) if (pi + 1) % 2 == 0 else (xT2, xT2B)
            elif own_prefetched[0]:
                nxt_buf = (xT, xTB)
            ffn(T, C_G1, "w1a", "w3a", "w2a", bg=pending[0], xt=xt, xtB=xtB, stats_done=stats_ready[0])
            pending[0] = None
            stats_ready[0] = False
            rmsnorm(T, C_GM, lambda c: hT[:, c, 0:T], hTB, xt, xtB)
            proj(T, 1536, lambda pp: evac_copy(rj[:, 0:T], PS[pp][:, 0:T], [PB[pp]], [rjB]), split=True)
            for pr in range(2):
                proj(T, 256 + pr * 128, lambda pp, pr=pr: evac_copy(k_sb[:, pr, 0:T], PS[pp][:, 0:T], [PB[pp]], [ksB[pr]]))
            if nxt_buf is not None:
                rmsnorm(TP if pi + 1 < n_pre else tiles[0][1], C_G1, None, None, nxt_buf[0], nxt_buf[1], stats=True, apply=False)
                stats_ready[0] = True
            if last:
                for c in range(4):
                    proj(T, 1552 + c * 128, lambda pp, c=c: evac_copy(uext[:, c, 16:16 + T], PS[pp][:, 0:T], [PB[pp]], [uextB]))
                P.op("dve", lambda e: e.tensor_copy(halo[:], uext[:, :, T:T + 16]), reads=[uextB, haloB], writes=[haloB])
            v_tokmajor(T)

            def tail():
                t = lambda i: tmp[:, i, 0:T]
                for pr in range(2):
                    log_decay(T, pr, pl=pr)
                    P.op("dve", lambda e: e.tensor_tensor_scan(t(0), cc(C_ONE).to_broadcast([128, T]), t(1), 0.0, ALU.mult, ALU.add),
                         reads=[cstB, tB[1]], writes=[tB[0]])
                    P.op("dve", lambda e: e.tensor_tensor(t(2), tmp[:, 0, T - 1:T].to_broadcast([128, T]), t(0), ALU.subtract),
                         reads=[tB[0]], writes=[tB[2]])
                    P.op("act", lambda e: e.activation(out=t(3), in_=t(2), func=AF.Exp), reads=[tB[2]], writes=[tB[3]])
                    P.op("act", lambda e, pr=pr: e.activation(out=dT[:, pr:pr + 1], in_=tmp[:, 0, T - 1:T], func=AF.Exp),
                         reads=[tB[0]], writes=[dTB])
                    P.op("dve", lambda e, pr=pr: e.tensor_tensor(ke[:, pr, 0:T], k_sb[:, pr, 0:T], t(3), ALU.mult),
                         reads=[ksB[pr], tB[3]], writes=[keB[pr]])

                    def trk(e, pr=pr):
                        r = None
                        for g in range(TG):
                            r = e.matmul(PS[2][0:tg, g * 128:(g + 1) * 128], ke[:, pr, g * tg:(g + 1) * tg], ident_r[:], start=True, stop=True)
                        return r
                    P.op("pe", trk, reads=[keB[pr], identrB], writes=[PB[2]])
                    evac_copy(kTt[0:tg, 0:TG, :], PS[2][0:tg, 0:TG * 128].rearrange("p (g c) -> p g c", g=TG), [PB[2]], kTtB)

                    def mmu(e, pr=pr):
                        r = None
                        for g in range(TG):
                            r = e.matmul(PS[3][:, 0:256], kTt[0:tg, g, :], vtm[0:tg, g, pr * 256:(pr + 1) * 256], start=(g == 0), stop=(g == TG - 1))
                        return r
                    P.op("pe", mmu, reads=kTtB + vtmB[0:TG], writes=[PB[3]])
                    P.op("dve", lambda e, pr=pr: e.scalar_tensor_tensor(Sr[:, pr, 0, :], Sr[:, pr, 0, :], dT[:, pr:pr + 1], PS[3][:, 0:256], ALU.mult, ALU.add),
                         reads=[SrB[pr][0], dTB, PB[3]], writes=[SrB[pr][0]])
            pending[0] = P.capture(tail)

        for pi_ in range(n_pre):
            prefix_tile(pi_)

        def set_state(src, srcB):
            for pr in range(2):
                slot[pr] = 0
                P.op("dve", lambda e, pr=pr: e.tensor_copy(Sr[:, pr, 0, :], src[:, pr * 256:(pr + 1) * 256]),
                     reads=[srcB], writes=[SrB[pr][0]])

        def own_tile(ti, r0, T, kind):
            tg = min(T, 128)
            TG = T // tg
            nb = T // 16
            nbg = tg // 16
            first_of_kind = (ti == 0) or (tiles[ti - 1][2] != kind)
            last_of_kind = (ti + 1 == len(tiles)) or (tiles[ti + 1][2] != kind)
            L = 16 + T
            if ti == 0:
                P.fence(xT2B, qmB + gateB)
            if not (ti == 0 and own_prefetched[0]):
                load_x(x_d, r0, T)
            ffn(T, C_G1, "w1a", "w3a", "w2a", bg=pending[0], stats_done=stats_ready[0])
            pending[0] = None
            stats_ready[0] = False
            rmsnorm(T, C_GM, lambda c: hT[:, c, 0:T], hTB)
            if kind == "s" and first_of_kind:
                P.op("sp", lambda e: e.dma_start(out=tmp[:, 6, :], in_=s0s_d), writes=[tB[6]], chan="c7")
                set_state(tmp[:, 6, :], tB[6])
                P.op("dve", lambda e: e.tensor_copy(halo[:], halos[:]), reads=[halosB, haloB], writes=[haloB])
            proj(T, 1536, lambda pp: evac_copy(rj[:, 0:T], PS[pp][:, 0:T], [PB[pp]], [rjB]), split=True)
            P.fence(osB, qsB + ksB)
            t = lambda i: tmp[:, i, 0:T]

            def prep(pr):
                log_decay(T, pr)
                P.op("dve", lambda e: e.tensor_tensor_scan(t(2), smask[:, 0:T], t(1), 0.0, ALU.mult, ALU.add),
                     reads=[smaskB, tB[1]], writes=[tB[2]])
                b3 = tmp[:, 2, 0:T].rearrange("p (n l) -> p n l", l=16)
                P.op("act", lambda e: e.activation(out=t(3), in_=t(2), func=AF.Exp), reads=[tB[2]], writes=[tB[3]])
                P.op("act", lambda e: e.activation(out=t(4), in_=t(2), func=AF.Exp, scale=-1.0), reads=[tB[2]], writes=[tB[4]])
                P.op("dve", lambda e: e.tensor_tensor(tmp[:, 5, 0:T].rearrange("p (n l) -> p n l", l=16),
                                                      b3[:, :, 15:16].to_broadcast([128, nb, 16]), b3, ALU.subtract),
                     reads=[tB[2]], writes=[tB[5]])
                P.op("act", lambda e: e.activation(out=t(6), in_=t(5), func=AF.Exp), reads=[tB[5]], writes=[tB[6]])
                P.op("act", lambda e: e.activation(out=d16[:, pr, 0:nb], in_=b3[:, :, 15], func=AF.Exp),
                     reads=[tB[2]], writes=[d16B])
                for hl in range(2):
                    h = 2 * pr + hl
                    P.op("dve", lambda e, h=h, hl=hl: e.scalar_tensor_tensor(qm[:, h, 0:T], q_sb[:, pr, 0:T], cc(C_HM + hl), t(3), ALU.mult, ALU.mult),
                         reads=[qsB[pr], tB[3], cstB], writes=[qmB[h]])
                P.op("dve", lambda e: e.tensor_tensor(kt[:, pr, 0:T], k_sb[:, pr, 0:T], t(4), ALU.mult),
                     reads=[ksB[pr], tB[4]], writes=[ktB[pr]])
                P.op("dve", lambda e: e.tensor_tensor(ke[:, pr, 0:T], k_sb[:, pr, 0:T], t(6), ALU.mult),
                     reads=[ksB[pr], tB[6]], writes=[keB[pr]])
            for pr in range(2):
                proj(T, pr * 128, lambda pp, pr=pr: evac_copy(q_sb[:, pr, 0:T], PS[pp][:, 0:T], [PB[pp]], [qsB[pr]]))
                proj(T, 256 + pr * 128, lambda pp, pr=pr: evac_copy(k_sb[:, pr, 0:T], PS[pp][:, 0:T], [PB[pp]], [ksB[pr]]))
                prep(pr)
            v_tokmajor(T)
            for c in range(4):
                proj(T, 1024 + c * 128, lambda pp, c=c: P.op("act", lambda e: e.activation(out=gate[:, c, 0:T], in_=PS[pp][:, 0:T], func=AF.Silu),
                                                             reads=[PB[pp]], writes=[gateB[c]]))
            for c in range(4):
                proj(T, 1552 + c * 128, lambda pp, c=c: evac_copy(uext[:, c, 16:L], PS[pp][:, 0:T], [PB[pp]], [uextB]))
            P.op("dve", lambda e: e.tensor_copy(uext[:, :, 0:16], halo[:]), reads=[haloB, uextB], writes=[uextB])
            P.fence(qsB + ksB, osB)

            UB = [PB[0], PB[1], PB[4], PB[5]]
            US = [PS[0][:, 0:256], PS[1][:, 0:256], PS[4][:, 0:256], PS[5][:, 0:256]]
            ucnt = [0]
            sl_of = {}

            def kem_attn(g):
                ts0 = g * tg
                kb = g % 2
                kem, kemB, attm, attmB = kem2[kb], kem2B[kb], attm2[kb], attm2B[kb]

                def trk(e):
                    r = None
                    for pr in range(2):
                        r = e.matmul(PS[2][0:tg, pr * 128:(pr + 1) * 128], ke[:, pr, ts0:ts0 + tg], ident_r[:], start=True, stop=True)
                    return r
                P.op("pe", trk, reads=keB + [identrB], writes=[PB[2]])
                for pr in range(2):
                    for n in range(nbg):
                        P.op("act", lambda e, pr=pr, n=n: e.activation(out=kem[0:tg, pr, n, :], in_=PS[2][0:tg, pr * 128:(pr + 1) * 128], func=AF.Copy,
                                                                       scale=cst[0:tg, C_BM + n:C_BM + n + 1]),
                             reads=[PB[2], cstB], writes=[kemB])
                for h in range(4):
                    pr = h // 2
                    P.op("pe", lambda e, h=h, pr=pr: e.matmul(PS[3][0:tg, h * 128:h * 128 + tg], kt[:, pr, ts0:ts0 + tg], qm[:, h, ts0:ts0 + tg], start=True, stop=True),
                         reads=[ktB[pr], qmB[h]], writes=[PB[3]])
                for h in range(4):
                    P.op("dve", lambda e, h=h: e.tensor_tensor(attm[0:tg, h, 0:tg], PS[3][0:tg, h * 128:h * 128 + tg], cmask[0:tg, 0:tg], ALU.mult),
                         reads=[PB[3], cmaskB], writes=[attmB[h]])

            def chain(g):
                kb = g % 2
                kem, kemB = kem2[kb], kem2B[kb]
                sl0 = list(slot)
                sl_of[g] = sl0
                for n in range(nbg):
                    blk = g * nbg + n
                    for pr in range(2):
                        ui = ucnt[0] % 4
                        ucnt[0] += 1
                        s_cur = (sl0[pr] + n) % NSL
                        s_nxt = (sl0[pr] + n + 1) % NSL
                        P.op("pe", lambda e, pr=pr, n=n, ui=ui: e.matmul(US[ui], kem[0:tg, pr, n, :], vtm[0:tg, g, pr * 256:(pr + 1) * 256], start=True, stop=True),
                             reads=[kemB, vtmB[g]], writes=[UB[ui]])
                        P.op("dve", lambda e, pr=pr, ui=ui, s_cur=s_cur, s_nxt=s_nxt, blk=blk: e.scalar_tensor_tensor(
                            Sr[:, pr, s_nxt, :], Sr[:, pr, s_cur, :], d16[:, pr, blk:blk + 1], US[ui], ALU.mult, ALU.add),
                            reads=[SrB[pr][s_cur], d16B, UB[ui]], writes=[SrB[pr][s_nxt]])
                for pr in range(2):
                    slot[pr] = (sl0[pr] + nbg) % NSL

            def outp(g):
                ts0 = g * tg
                kb = g % 2
                attm, attmB = attm2[kb], attm2B[kb]
                sl0 = sl_of[g]
                for h in range(4):
                    pr, hl = h // 2, h % 2
                    po = 6 + h % 2

                    def mmo(e, h=h, pr=pr, hl=hl, po=po):
                        r = e.matmul(PS[po][:, 0:tg], vtm[0:tg, g, h * 128:(h + 1) * 128], attm[0:tg, h, 0:tg], start=True, stop=False)
                        for n in range(nbg):
                            s_n = (sl0[pr] + n) % NSL
                            r = e.matmul(PS[po][:, n * 16:(n + 1) * 16], Sr[:, pr, s_n, hl * 128:(hl + 1) * 128],
                                         qm[:, h, ts0 + n * 16:ts0 + (n + 1) * 16], start=False, stop=(n == nbg - 1))
                        return r
                    P.op("pe", mmo, reads=[vtmB[g], attmB[h], qmB[h]] + [SrB[pr][(sl0[pr] + n) % NSL] for n in range(nbg)], writes=[PB[po]])
                    evac_copy(o_sb[:, h, ts0:ts0 + tg], PS[po][:, 0:tg], [PB[po]], [osB[h]])
            kem_attn(0)
            for g_ in range(TG):
                chain(g_)
                if g_ + 1 < TG:
                    kem_attn(g_ + 1)
                outp(g_)
            for h in range(4):
                s = h % 2
                P.op("act", lambda e, h=h, s=s: e.activation(out=sq[:, s, 0:T], in_=o_sb[:, h, 0:T], func=AF.Square), reads=[osB[h]], writes=[sqB[s]])
                P.op("pe", lambda e, s=s: e.matmul(PS[0][:, 0:T], ones_r[:], sq[:, s, 0:T], start=True, stop=True), reads=[onesB, sqB[s]], writes=[PB[0]])
                P.op("act", lambda e: e.activation(out=lnv[:, 0:T], in_=PS[0][:, 0:T], func=AF.Ln, scale=1.0 / 128, bias=cc(C_EPS)), reads=[PB[0], cstB], writes=[lnvB])
                P.op("act", lambda e: e.activation(out=rstd[:, 0:T], in_=lnv[:, 0:T], func=AF.Exp, scale=-0.5), reads=[lnvB], writes=[rstdB])
                P.op("dve", lambda e, h=h: e.scalar_tensor_tensor(onn[:, 0:T], o_sb[:, h, 0:T], cc(C_GLAN), rstd[:, 0:T], ALU.mult, ALU.mult),
                     reads=[osB[h], rstdB, cstB], writes=[onnB])
                P.op("dve", lambda e, h=h: e.tensor_tensor(mixed[:, h, 0:T], onn[:, 0:T], gate[:, h, 0:T], ALU.mult), reads=[onnB, gateB[h]], writes=[mixB[h]])
            for gi in range(4):
                src, srcBuf = uext[:, gi, :], uextB
                for lv in range(gi + 1):
                    sh = 1 << lv
                    lo = 2 * sh - 1
                    dst, dstBuf = (Ea, EaB) if lv % 2 == 0 else (Eb, EbB)
                    P.op("dve", lambda e, src=src, dst=dst, sh=sh, lo=lo: e.tensor_tensor(dst[:, lo:L], src[:, lo:L], src[:, lo - sh:L - sh], ALU.add),
                         reads=[srcBuf], writes=[dstBuf])
                    src, srcBuf = dst, dstBuf
                zs = gi % 2
                P.op("dve", lambda e, gi=gi, src=src, zs=zs: e.scalar_tensor_tensor(zz[:, zs, 0:T], src[:, 16:L], cc(C_INVW + gi), uext[:, gi, 16:L], ALU.mult, ALU.subtract),
                     reads=[srcBuf, uextB, cstB], writes=[zzB[zs]])
                if kind == "p" and first_of_kind:
                    P.op("dve", lambda e, gi=gi, src=src: e.tensor_tensor(zt[:, :], src[:, 16:32], invcnt[:, gi, :], ALU.mult), reads=[srcBuf, invcntB], writes=[ztB])
                    P.op("dve", lambda e, gi=gi, zs=zs: e.tensor_tensor(zz[:, zs, 0:16], zt[:, :], uext[:, gi, 16:32], ALU.subtract), reads=[ztB, uextB], writes=[zzB[zs]])
                P.op("pe", lambda e, gi=gi, zs=zs: e.matmul(PS[gi % 4][:, 0:T], wpool_r[:, gi, :], zz[:, zs, 0:T], start=True, stop=True), reads=[wpoolrB, zzB[zs]], writes=[PB[gi % 4]])
                P.op("act", lambda e, gi=gi: e.activation(out=mixed[:, 4 + gi, 0:T], in_=PS[gi % 4][:, 0:T], func=AF.Copy, scale=cc(C_PS + gi)),
                     reads=[PB[gi % 4], cstB], writes=[mixB[4 + gi]])
            P.op("dve", lambda e: e.tensor_copy(halo[:], uext[:, :, T:T + 16]), reads=[uextB, haloB], writes=[haloB])
            if last_of_kind:
                ko = 0 if kind == "p" else 1
                P.op("sp", lambda e: e.dma_start(out=tail_d[ko], in_=halo[:, :, 1:16]), reads=[haloB], chan="otl")
                for pr in range(2):
                    s_now = slot[pr]
                    P.op("sp", lambda e, pr=pr, s_now=s_now: e.dma_start(out=sfin_d[ko, :, pr * 256:(pr + 1) * 256], in_=Sr[:, pr, s_now, :]),
                         reads=[SrB[pr][s_now]], chan="osf%d" % pr)
            for oc in range(8):
                blk, bB = WS.get(("cols", "w_out", oc * 128))
                pp = 4 + oc % 4

                def mmw(e, blk=blk, pp=pp):
                    r = None
                    for c in range(8):
                        r = e.matmul(PS[pp][:, 0:T], blk[:, c, :], mixed[:, c, 0:T], start=(c == 0), stop=(c == 7))
                    return r
                P.op("pe", mmw, reads=[bB] + mixB, writes=[PB[pp]])
                P.op("dve", lambda e, oc=oc, pp=pp: e.tensor_tensor(xT[:, oc, 0:T], PS[pp][:, 0:T], xT[:, oc, 0:T], ALU.add), reads=[PB[pp], xTB[oc]], writes=[xTB[oc]])
            ffn(T, C_G2, "w1b", "w3b", "w2b")
            yB = [osB[0], osB[1], osB[2], osB[3], uextB, uextB, uextB, uextB]
            P.fence(qsB + ksB, osB)
            yc = lambda c: qk_sb[:, c, 0:T] if c < 4 else uext[:, c - 4, 0:T]
            rmsnorm(T, C_GF, yc, yB)
            yv = y_d[:, r0 * 8:(r0 + T) * 8].rearrange("p (c t) -> p c t", c=8)
            P.op("act", lambda e: e.dma_start(out=yv[:, 0:4, :], in_=qk_sb[:, :, 0:T]), reads=osB, chan="oy0")
            P.op("act", lambda e: e.dma_start(out=yv[:, 4:8, :], in_=uext[:, :, 0:T]), reads=[uextB], chan="oy1")

        for ti_, (r0_, T_, kind_) in enumerate(tiles):
            own_tile(ti_, r0_, T_, kind_)
        if not dry:
            P.emit()
    return nc, WS.rec


_CACHE = {}


def get_prog(n_pre, tiles):
    key = (n_pre, tuple(tiles))
    if key not in _CACHE:
        _, plan = build(n_pre, tiles, dry_plan=None, dry=True)
        nc, _ = build(n_pre, tiles, dry_plan=plan, dry=False)
        _CACHE[key] = nc
    return _CACHE[key]

def make_consts(core, n_chunks, inp):
    cst = np.zeros((128, NCST), np.float32)
    f = lambda v: np.asarray(v, np.float32)
    cst[:, C_G1:C_G1 + 8] = f(inp["norm_ffn1"])[0].reshape(8, 128).T
    cst[:, C_GM:C_GM + 8] = f(inp["norm_mix"])[0].reshape(8, 128).T
    cst[:, C_G2:C_G2 + 8] = f(inp["norm_ffn2"])[0].reshape(8, 128).T
    cst[:, C_GF:C_GF + 8] = f(inp["norm_final"]).reshape(8, 128).T
    cst[:, C_BG:C_BG + 2] = f(inp["b_gate"])[0].reshape(2, 128).T
    cst[:, C_GLAN] = f(inp["gla_norm"])[0]
    cst[:, C_PS:C_PS + 4] = f(inp["pool_scale"])[0].reshape(4, 128).T
    cst[:, C_EPS] = 1e-6
    cst[:, C_ONE] = 1.0
    cst[0:64, C_HM] = 0.125
    cst[64:128, C_HM + 1] = 0.125
    for n in range(8):
        cst[16 * n:16 * (n + 1), C_BM + n] = 1.0
    cst[:, C_SEL + core] = 1.0
    if core % n_chunks != 0:
        cst[:, C_SELP + core - 1] = 1.0
    for gi, w in enumerate((2, 4, 8, 16)):
        cst[:, C_INVW + gi] = 1.0 / w
    return cst


def kernel(x_prompt, x_sample, state_gla, cache_pool, w_in, w_gate_up, b_gate, gla_norm, w_pool,
           pool_scale, w_out, norm_ffn1, w1_ffn1, w3_ffn1, w2_ffn1, norm_mix, norm_ffn2, w1_ffn2,
           w3_ffn2, w2_ffn2, norm_final):
    inp = dict(norm_ffn1=norm_ffn1, norm_mix=norm_mix, norm_ffn2=norm_ffn2, norm_final=norm_final,
               b_gate=b_gate, gla_norm=gla_norm, pool_scale=pool_scale)
    f = lambda v: np.ascontiguousarray(np.asarray(v, np.float32))
    x_prompt, x_sample = f(x_prompt), f(x_sample)
    B, S, _ = x_prompt.shape
    NB_S, TS, _ = x_sample.shape
    n_chunks = NCORE // B
    CH = S // n_chunks
    TP = 512
    tiles = [(i * TP, TP, "p") for i in range(CH // TP)] + [(CH, TS, "s")]
    n_pre = (n_chunks - 1) * CH // TP
    ident = np.eye(128, dtype=np.float32)
    smask = np.ones((128, 512), np.float32)
    smask[:, ::16] = 0.0
    jj, tt = np.meshgrid(np.arange(128), np.arange(128), indexing="ij")
    cmask = ((jj // 16 == tt // 16) & (jj <= tt)).astype(np.float32)
    wgu = np.zeros((128, 2, 128), np.float32)
    wgu[0:16] = f(w_gate_up)[0].reshape(16, 2, 128)
    wpool = np.ascontiguousarray(f(w_pool)[0].transpose(1, 0, 2))
    sg = f(state_gla)[0]
    cp = f(cache_pool)[0]
    nc = get_prog(n_pre, tiles)
    maps = []
    def blk_cols(w, c0):
        return w[:, c0:c0 + 128].reshape(8, 128, 128).transpose(1, 0, 2).reshape(128, 1024)

    def blk_rows(w, kp, c0):
        return w[kp * 256:(kp + 1) * 256, c0:c0 + 512].reshape(2, 128, 512).transpose(1, 0, 2).reshape(128, 1024)

    def blk_w2(w, j, half):
        return w[j * 256:(j + 1) * 256, half * 512:(half + 1) * 512].reshape(2, 128, 512).transpose(1, 0, 2).reshape(128, 1024)
    WIN_COLS = [1536, 0, 128, 256, 384, 1024, 1152, 1280, 1408, 1552, 1680, 1808, 1936]
    win = f(w_in)[0]
    wts = {}
    for nm, w in (("w1a", w1_ffn1), ("w3a", w3_ffn1), ("w1b", w1_ffn2), ("w3b", w3_ffn2)):
        w = f(w)[0]
        wts[nm] = np.ascontiguousarray(np.stack([blk_cols(w, fc * 128) for fc in range(NFC)], 0))
    for nm, w in (("w2a", w2_ffn1), ("w2b", w2_ffn2)):
        w = f(w)[0]
        wts[nm] = np.ascontiguousarray(np.stack([blk_w2(w, i // 2, i % 2) for i in range(NFC)], 0))
    wts["w_in"] = np.ascontiguousarray(np.stack([blk_cols(win, c0) for c0 in WIN_COLS] + [blk_rows(win, kp, 512) for kp in range(4)], 0))
    wts["w_out"] = np.ascontiguousarray(np.stack([blk_cols(f(w_out)[0], oc * 128) for oc in range(8)], 0))
    def to_fm(rows, tl):
        return np.ascontiguousarray(np.concatenate(
            [rows[r0:r0 + T].reshape(T, 8, 128).transpose(2, 1, 0).reshape(128, 8 * T) for (r0, T, _) in tl], 1))

    def from_fm(a, tl):
        return np.concatenate([a[:, r0 * 8:(r0 + T) * 8].reshape(128, 8, T).transpose(2, 1, 0).reshape(T, 1024) for (r0, T, _) in tl], 0)
    pre_tiles = [(i * TP, TP, "p") for i in range(max(n_pre, 1))]
    for c in range(NCORE):
        b, j = c // n_chunks, c % n_chunks
        x = to_fm(np.concatenate([x_prompt[b, j * CH:(j + 1) * CH], x_sample[c]], 0), tiles)
        xp = np.zeros((max(n_pre, 1) * TP, D), np.float32)
        if j > 0:
            xp[(n_chunks - 1 - j) * CH:(n_chunks - 1) * CH] = x_prompt[b, 0:j * CH]
        xp = to_fm(xp, pre_tiles)
        invcnt = np.zeros((128, 4, 16), np.float32)
        for gi, w in enumerate((2, 4, 8, 16)):
            pos = j * CH + np.arange(16)
            invcnt[:, gi, :] = 1.0 / np.minimum(w, pos + 1)
        s0s = np.zeros((128, 2, 2, 128), np.float32)
        for h in range(4):
            pr, hl = h // 2, h % 2
            s0s[hl * 64:(hl + 1) * 64, pr, hl, :] = sg[c, h]
        halos = np.zeros((128, 4, 16), np.float32)
        halos[:, :, 1:16] = cp[c].reshape(15, 4, 128).transpose(2, 1, 0)
        maps.append(dict(cst=make_consts(c, n_chunks, inp), ident=ident, xp=xp, x=x, wgu=wgu, smask=smask, wpool=wpool,
                         cmask=cmask, invcnt=invcnt, s0s=s0s.reshape(128, 512), halos=halos, **wts))
    res = run_bass_kernel_spmd(nc, maps, core_ids=list(range(NCORE))).results
    y_prompt = np.zeros((B, S, D), np.float32)
    y_sample = np.zeros((NB_S, TS, D), np.float32)
    st_p = np.zeros((1, B, 4, 64, 128), np.float32)
    cp_p = np.zeros((1, B, 15, 512), np.float32)
    st_s = np.zeros((1, NB_S, 4, 64, 128), np.float32)
    cp_s = np.zeros((1, NB_S, 15, 512), np.float32)

    def unstate(a):
        a = a.reshape(128, 2, 2, 128)
        return np.stack([a[(h % 2) * 64:(h % 2 + 1) * 64, h // 2, h % 2, :] for h in range(4)], 0)

    def untail(a):
        return a.transpose(2, 1, 0).reshape(15, 512)
    for c in range(NCORE):
        b, j = c // n_chunks, c % n_chunks
        y = from_fm(res[c]["y"], tiles)
        y_prompt[b, j * CH:(j + 1) * CH] = y[0:CH]
        y_sample[c] = y[CH:CH + TS]
        st_s[0, c] = unstate(res[c]["sfin"][1])
        cp_s[0, c] = untail(res[c]["tail"][1])
        if j == n_chunks - 1:
            st_p[0, b] = unstate(res[c]["sfin"][0])
            cp_p[0, b] = untail(res[c]["tail"][0])
    return (y_prompt, y_sample, st_p, cp_p, st_s, cp_s)
```
